# Optimizing a Trainium2 kernel written in Bass

```python
import jax, jax.numpy as jnp
from jax import lax
import numpy as np


D_MODEL = 2048
BATCH = 16
SEQ = 256
DEPTH = 2
DEC_BATCH = 4
DEC_SEQ = 4096
PAST_LEN = 512

GRID_W = 64
D_CONV_A = 1024
CONV_A_WIDTH = 3
N_HEADS = 16
QK_NOPE_DIM = 128
QK_ROPE_DIM = 64
V_HEAD_DIM = 128
KV_LORA_RANK = 512
QK_HEAD_DIM = QK_NOPE_DIM + QK_ROPE_DIM
N_FREQ = QK_ROPE_DIM // 4
ROPE_BASE = 10000.0
ATTN_SCALE = QK_HEAD_DIM ** -0.5
Q_BLOCK = 128
D_CONF = 1024
CONF_WIDTH = 31
N_BRANCHES = 3
D_FF = 4 * D_MODEL
NORM_EPS = 1e-6

SPLIT_A = 3 * D_CONV_A
SPLIT_Q = N_HEADS * QK_HEAD_DIM
SPLIT_KV = KV_LORA_RANK + QK_ROPE_DIM
SPLIT_CONF = 2 * D_CONF
SPLIT_GATE = N_BRANCHES * D_MODEL
OFF_Q = SPLIT_A
OFF_KV = OFF_Q + SPLIT_Q
OFF_CONF = OFF_KV + SPLIT_KV
OFF_GATE = OFF_CONF + SPLIT_CONF
D_IN_PROJ = OFF_GATE + SPLIT_GATE

kernel_name = 'hybrid_diffusion_parallel_gated_mla_conv_step'


def rms_norm(x, g):
    xf = x.astype(jnp.float32)
    y = xf * lax.rsqrt(jnp.mean(xf * xf, axis=-1, keepdims=True) + NORM_EPS)
    return (y * g.astype(jnp.float32)).astype(x.dtype)


def layer_norm(x, g, b):
    xf = x.astype(jnp.float32)
    mu = jnp.mean(xf, axis=-1, keepdims=True)
    xc = xf - mu
    var = jnp.mean(xc * xc, axis=-1, keepdims=True)
    y = xc * lax.rsqrt(var + NORM_EPS) * g.astype(jnp.float32) + b.astype(jnp.float32)
    return y.astype(x.dtype)


def depthwise_conv(x, w):
    k = w.shape[0]
    pad = (k - 1) // 2
    return lax.conv_general_dilated(
        x, w[:, None, :].astype(x.dtype), window_strides=(1,), padding=[(pad, pad)],
        dimension_numbers=('NWC', 'WIO', 'NWC'), feature_group_count=x.shape[-1])


def axial_rope_tables(n_tokens):
    rows = n_tokens // GRID_W
    row = jnp.repeat(jnp.arange(rows, dtype=jnp.float32), GRID_W)
    col = jnp.tile(jnp.arange(GRID_W, dtype=jnp.float32), rows)
    inv_freq = ROPE_BASE ** (-jnp.arange(N_FREQ, dtype=jnp.float32) / N_FREQ)
    ang = jnp.stack([row[:, None] * inv_freq, col[:, None] * inv_freq], axis=1)
    return jnp.cos(ang), jnp.sin(ang)


def apply_axial_rope(x, cos, sin):
    shp = x.shape
    xr = x.astype(jnp.float32).reshape(shp[:-1] + (2, 2, N_FREQ))
    x1, x2 = xr[..., 0, :], xr[..., 1, :]
    c, s = cos[:, None], sin[:, None]
    out = jnp.stack([x1 * c - x2 * s, x2 * c + x1 * s], axis=-2)
    return out.reshape(shp).astype(x.dtype)


def blocked_attention(q_nope, q_pe, k_nope, k_pe, v):
    b, sq, h, _ = q_nope.shape
    nblk = sq // Q_BLOCK

    def to_blocks(t):
        return jnp.moveaxis(t.reshape((b, nblk, Q_BLOCK) + t.shape[2:]), 1, 0)

    def one_block(qs):
        qn, qp = qs
        s = (jnp.einsum('bqhd,bkhd->bhqk', qn, k_nope, preferred_element_type=jnp.float32)
             + jnp.einsum('bqhr,bkr->bhqk', qp, k_pe, preferred_element_type=jnp.float32)) * ATTN_SCALE
        p = jax.nn.softmax(s, axis=-1).astype(v.dtype)
        return jnp.einsum('bhqk,bkhd->bqhd', p, v)

    out = lax.map(one_block, (to_blocks(q_nope), to_blocks(q_pe)))
    return jnp.moveaxis(out, 0, 1).reshape(b, sq, h, V_HEAD_DIM)


def short_conv_mixer(a_all, conv_a_w, w_out_a):
    xin = a_all[..., :D_CONV_A]
    gate_b = a_all[..., D_CONV_A:2 * D_CONV_A]
    gate_c = a_all[..., 2 * D_CONV_A:]
    y = depthwise_conv(gate_c * xin, conv_a_w)
    return (gate_b * y) @ w_out_a


def mla_mixer(q_all, kv_all, rope, ctx_ckv, ctx_kpe, kv_norm_g, w_kv_b, w_o_attn):
    b, s, _ = q_all.shape
    q = q_all.reshape(b, s, N_HEADS, QK_HEAD_DIM)
    q_nope, q_pe = q[..., :QK_NOPE_DIM], q[..., QK_NOPE_DIM:]
    ckv = rms_norm(kv_all[..., :KV_LORA_RANK], kv_norm_g)
    kpe = kv_all[..., KV_LORA_RANK:]
    if rope is None:
        keys_ckv, keys_kpe = ckv, kpe
    else:
        cos, sin = rope
        q_pe = apply_axial_rope(q_pe, cos, sin)
        kpe_rot = apply_axial_rope(kpe[:, :, None, :], cos, sin)[:, :, 0, :]
        keys_ckv = jnp.concatenate([ckv, ctx_ckv.astype(ckv.dtype)], axis=1)
        keys_kpe = jnp.concatenate([kpe_rot, ctx_kpe.astype(kpe.dtype)], axis=1)
    kv = jnp.einsum('bkc,chd->bkhd', keys_ckv, w_kv_b)
    k_nope, v = kv[..., :QK_NOPE_DIM], kv[..., QK_NOPE_DIM:]
    o = blocked_attention(q_nope, q_pe, k_nope, keys_kpe, v)
    return jnp.einsum('bshd,hdm->bsm', o, w_o_attn), ckv, kpe


def conformer_conv_mixer(c_all, conf_dw_w, conf_dw_b, conf_ln_g, conf_ln_b, w_out_c):
    u = jax.nn.glu(c_all, axis=-1)
    u = depthwise_conv(u, conf_dw_w) + conf_dw_b
    u = jax.nn.silu(layer_norm(u, conf_ln_g, conf_ln_b))
    return u @ w_out_c


def trunk_layer(x, mod, rope, ctx_ckv, ctx_kpe, norm1_g, w_in, conv_a_w, w_out_a, kv_norm_g,
                w_kv_b, w_o_attn, conf_dw_w, conf_dw_b, conf_ln_g, conf_ln_b, w_out_c,
                w_merge, norm2_g, w_ff1, w_ff2):
    mod = mod.astype(x.dtype)
    shift1, scale1, gate1, shift2, scale2, gate2 = jnp.split(mod, 6, axis=-1)
    h = rms_norm(x, norm1_g) * (1 + scale1) + shift1
    proj = h @ w_in
    y_a = short_conv_mixer(proj[..., :OFF_Q], conv_a_w, w_out_a)
    y_b, ckv, kpe = mla_mixer(proj[..., OFF_Q:OFF_KV], proj[..., OFF_KV:OFF_CONF], rope,
                              ctx_ckv, ctx_kpe, kv_norm_g, w_kv_b, w_o_attn)
    y_c = conformer_conv_mixer(proj[..., OFF_CONF:OFF_GATE], conf_dw_w, conf_dw_b,
                               conf_ln_g, conf_ln_b, w_out_c)
    g = jax.nn.sigmoid(proj[..., OFF_GATE:])
    merged = (g[..., :D_MODEL] * y_a + g[..., D_MODEL:2 * D_MODEL] * y_b
              + g[..., 2 * D_MODEL:] * y_c)
    x = x + gate1 * (merged @ w_merge)
    h2 = rms_norm(x, norm2_g) * (1 + scale2) + shift2
    x = x + gate2 * (jnp.square(jax.nn.relu(h2 @ w_ff1)) @ w_ff2)
    return x, ckv, kpe


def setup_inputs(seed: int = 0) -> dict:
    key = jax.random.key(seed)
    ks = jax.random.split(key, 28)

    def nrm(k, shape, scale=1.0):
        return jax.random.normal(k, shape, jnp.float32) * scale

    return {
        'x_prompt': nrm(ks[0], (BATCH, SEQ, D_MODEL)),
        'x_sample': nrm(ks[1], (DEC_BATCH, DEC_SEQ, D_MODEL)),
        'cache_ckv': nrm(ks[2], (DEC_BATCH, DEPTH, PAST_LEN, KV_LORA_RANK)),
        'cache_kpe': nrm(ks[3], (DEC_BATCH, DEPTH, PAST_LEN, QK_ROPE_DIM)),
        'c': nrm(ks[4], (DEC_BATCH, D_MODEL)),
        'c_ctx': nrm(ks[5], (D_MODEL,)),
        'w_mod': nrm(ks[6], (DEPTH, D_MODEL, 6 * D_MODEL), 0.5 * D_MODEL ** -0.5),
        'b_mod': nrm(ks[7], (DEPTH, 6 * D_MODEL), 0.02),
        'norm1_g': 1.0 + nrm(ks[8], (DEPTH, D_MODEL), 0.02),
        'w_in': nrm(ks[9], (DEPTH, D_MODEL, D_IN_PROJ), D_MODEL ** -0.5),
        'conv_a_w': nrm(ks[10], (DEPTH, CONV_A_WIDTH, D_CONV_A), CONV_A_WIDTH ** -0.5),
        'w_out_a': nrm(ks[11], (DEPTH, D_CONV_A, D_MODEL), D_CONV_A ** -0.5),
        'kv_norm_g': 1.0 + nrm(ks[12], (DEPTH, KV_LORA_RANK), 0.02),
        'w_kv_b': nrm(ks[13], (DEPTH, KV_LORA_RANK, N_HEADS, QK_NOPE_DIM + V_HEAD_DIM), KV_LORA_RANK ** -0.5),
        'w_o_attn': nrm(ks[14], (DEPTH, N_HEADS, V_HEAD_DIM, D_MODEL), (N_HEADS * V_HEAD_DIM) ** -0.5),
        'conf_dw_w': nrm(ks[15], (DEPTH, CONF_WIDTH, D_CONF), CONF_WIDTH ** -0.5),
        'conf_dw_b': nrm(ks[16], (DEPTH, D_CONF), 0.02),
        'conf_ln_g': 1.0 + nrm(ks[17], (DEPTH, D_CONF), 0.02),
        'conf_ln_b': nrm(ks[18], (DEPTH, D_CONF), 0.02),
        'w_out_c': nrm(ks[19], (DEPTH, D_CONF, D_MODEL), D_CONF ** -0.5),
        'w_merge': nrm(ks[20], (DEPTH, D_MODEL, D_MODEL), D_MODEL ** -0.5),
        'norm2_g': 1.0 + nrm(ks[21], (DEPTH, D_MODEL), 0.02),
        'w_ff1': nrm(ks[22], (DEPTH, D_MODEL, D_FF), D_MODEL ** -0.5),
        'w_ff2': nrm(ks[23], (DEPTH, D_FF, D_MODEL), D_FF ** -0.5),
        'final_norm_g': 1.0 + nrm(ks[24], (D_MODEL,), 0.02),
    }


def reference(x_prompt, x_sample, cache_ckv, cache_kpe, c, c_ctx, w_mod, b_mod, norm1_g, w_in,
              conv_a_w, w_out_a, kv_norm_g, w_kv_b, w_o_attn, conf_dw_w, conf_dw_b, conf_ln_g,
              conf_ln_b, w_out_c, w_merge, norm2_g, w_ff1, w_ff2, final_norm_g):
    def run_layer(l, x, mod, rope, ctx_ckv, ctx_kpe):
        return trunk_layer(x, mod, rope, ctx_ckv, ctx_kpe,
                           norm1_g=norm1_g[l], w_in=w_in[l], conv_a_w=conv_a_w[l],
                           w_out_a=w_out_a[l], kv_norm_g=kv_norm_g[l], w_kv_b=w_kv_b[l],
                           w_o_attn=w_o_attn[l], conf_dw_w=conf_dw_w[l], conf_dw_b=conf_dw_b[l],
                           conf_ln_g=conf_ln_g[l], conf_ln_b=conf_ln_b[l], w_out_c=w_out_c[l],
                           w_merge=w_merge[l], norm2_g=norm2_g[l], w_ff1=w_ff1[l], w_ff2=w_ff2[l])

    xp = x_prompt
    ckv_layers, kpe_layers = [], []
    for l in range(DEPTH):
        mod = (jax.nn.silu(c_ctx) @ w_mod[l] + b_mod[l])[None, None, :]
        xp, ckv, kpe = run_layer(l, xp, mod, None, None, None)
        ckv_layers.append(ckv)
        kpe_layers.append(kpe)
    y_prompt = rms_norm(xp, final_norm_g)
    new_ckv = jnp.stack(ckv_layers, axis=1)
    new_kpe = jnp.stack(kpe_layers, axis=1)

    rope = axial_rope_tables(x_sample.shape[1])
    xs = x_sample
    for l in range(DEPTH):
        mod = (jax.nn.silu(c) @ w_mod[l] + b_mod[l])[:, None, :]
        xs, _, _ = run_layer(l, xs, mod, rope, cache_ckv[:, l], cache_kpe[:, l])
    y_sample = rms_norm(xs, final_norm_g)

    return (y_prompt, y_sample, new_ckv, new_kpe)
```

```python
import contextlib
import numpy as np
import concourse.bass as bass
import concourse.mybir as mybir
from concourse.bass_utils import run_bass_kernel_spmd

F32 = mybir.dt.float32
BF16 = mybir.dt.bfloat16
AF = mybir.ActivationFunctionType
ALU = mybir.AluOpType

D = 2048
KC = 16
NTOK = 2560
NSAMP = 2048
TT = 512
NT = 5
DEPTH = 2
OFF_Q = 3072
OFF_KV = 6144
OFF_CONF = 6720
OFF_GATE = 8768
EPS = 1e-6
ATTN_SCALE = 192 ** -0.5
NKEY = 4608
PW = 10240 + 512
SL = 428
NSM = 2 * SL + 18
O_N1G, O_N2G, O_BMOD, O_KVG, O_CAW, O_CDW, O_CDB, O_CLG, O_CLB = 0, 16, 32, 128, 132, 156, 404, 412, 420
O_FNG, O_MASKL, O_MASKR = 2 * SL, 2 * SL + 16, 2 * SL + 17

ENG = ("pe", "act", "dve", "pool", "sp")
SAME_ENGINE_SYNC = ("act", "dve", "pool")


class T:
    __slots__ = ("ap", "lw", "rd")

    def __init__(self, ap=None):
        self.ap = ap
        self.lw = []
        self.rd = {}


class Prog:
    def __init__(self, nc, stack, n_dma_sems=(("sp", 28), ("pool", 28), ("act", 8))):
        self.nc = nc
        self.ops = {e: [] for e in ENG}
        self.tick = {e: 0 for e in ENG}
        self.known = {e: {} for e in ENG}
        self.sem = {}
        for e in ENG:
            self.sem[e] = stack.enter_context(nc.semaphore("s_" + e))
        self.slots = {}
        self.cnt = {}
        self.amt = {}
        self.rr = {}
        for q, n in n_dma_sems:
            self.slots[q] = []
            for i in range(n):
                key = "d_%s_%d" % (q, i)
                self.sem[key] = stack.enter_context(nc.semaphore(key))
                self.slots[q].append(key)
                self.cnt[key] = 0
                self.amt[key] = 16
            self.rr[q] = 0
        self.sem["cc"] = stack.enter_context(nc.semaphore("cc"))
        self.cnt["cc"] = 0
        self.amt["cc"] = 1

    def _need(self, e, deps):
        kn = self.known[e]
        best = {}
        for (k, v) in deps:
            if k == e and e not in SAME_ENGINE_SYNC:
                continue
            if kn.get(k, 0) >= v:
                continue
            if best.get(k, 0) < v:
                best[k] = v
        for k, v in best.items():
            kn[k] = v
        return list(best.items())

    @staticmethod
    def _deps(reads, writes):
        deps = []
        for t in reads:
            deps.extend(t.lw)
        for t in writes:
            deps.extend(t.lw)
            deps.extend(t.rd.items())
        return deps

    @staticmethod
    def _mark(ev, reads, writes):
        k, v = ev
        for t in reads:
            if t.rd.get(k, 0) < v:
                t.rd[k] = v
        for t in writes:
            t.lw = [ev]
            t.rd = {}

    def op(self, e, fn, reads=(), writes=()):
        waits = self._need(e, self._deps(reads, writes))
        self.tick[e] += 1
        ev = (e, self.tick[e])
        self.ops[e].append((waits, fn, (e, 1)))
        self._mark(ev, reads, writes)
        return ev

    def dma(self, q, out_ap, in_ap, reads=(), writes=()):
        slots = self.slots[q]
        key = slots[self.rr[q] % len(slots)]
        self.rr[q] += 1
        deps = self._deps(reads, writes)
        if self.cnt[key] > 0:
            deps.append((key, 16 * self.cnt[key]))
        waits = self._need(q, deps)
        self.cnt[key] += 1
        ev = (key, 16 * self.cnt[key])

        def fn(eng, out_ap=out_ap, in_ap=in_ap):
            return eng.dma_start(out=out_ap, in_=in_ap)
        self.ops[q].append((waits, fn, (key, 16)))
        self._mark(ev, reads, writes)
        return ev

    def dma_multi(self, q, pairs, reads=(), writes=()):
        deps = self._deps(reads, writes)
        evs = []
        for (out_ap, in_ap) in pairs:
            slots = self.slots[q]
            key = slots[self.rr[q] % len(slots)]
            self.rr[q] += 1
            d = list(deps)
            if self.cnt[key] > 0:
                d.append((key, 16 * self.cnt[key]))
            waits = self._need(q, d)
            self.cnt[key] += 1
            ev = (key, 16 * self.cnt[key])
            evs.append(ev)

            def fn(eng, out_ap=out_ap, in_ap=in_ap):
                return eng.dma_start(out=out_ap, in_=in_ap)
            self.ops[q].append((waits, fn, (key, 16)))
        for ev in evs:
            k, v = ev
            for t in reads:
                if t.rd.get(k, 0) < v:
                    t.rd[k] = v
        for t in writes:
            t.lw = list(evs)
            t.rd = {}
        return evs

    def collective(self, fn):
        waits = self._need("pool", [("cc", self.cnt["cc"])]) if self.cnt["cc"] > 0 else []
        self.cnt["cc"] += 1
        self.ops["pool"].append((waits, fn, ("cc", 1)))

    def barrier(self):
        evs = [(e, self.tick[e]) for e in ENG if self.tick[e] > 0]
        for key, c in self.cnt.items():
            if c > 0:
                evs.append((key, self.amt[key] * c))
        for e in ENG:
            waits = self._need(e, list(evs))
            if waits:
                self.ops[e].append((waits, None, None))

    def emit(self, block):
        prog = self

        def run(e, eng):
            for (waits, fn, inc) in prog.ops[e]:
                for (k, v) in waits:
                    eng.wait_ge(prog.sem[k], v)
                if fn is None:
                    continue
                ins = fn(eng)
                if inc is not None:
                    ins.then_inc(prog.sem[inc[0]], inc[1])

        @block.tensor
        def _(eng):
            run("pe", eng)

        @block.scalar
        def _(eng):
            run("act", eng)

        @block.vector
        def _(eng):
            run("dve", eng)

        @block.gpsimd
        def _(eng):
            run("pool", eng)

        @block.sync
        def _(eng):
            run("sp", eng)


class Ring:
    def __init__(self, aps):
        self.t = [T(a) for a in aps]
        self.i = 0

    def next(self):
        t = self.t[self.i % len(self.t)]
        self.i += 1
        return t


def build_program(L=DEPTH, n_cores=8, debug=False, stop=99):
    nc = bass.Bass("TRN2", target_bir_lowering=False)

    def din(name, shape, dt=F32):
        return nc.dram_tensor(name, list(shape), dt, kind="ExternalInput").ap()

    def dout(name, shape, dt=F32):
        return nc.dram_tensor(name, list(shape), dt, kind="ExternalOutput").ap()

    def dscr(name, shape, dt):
        if debug:
            return nc.dram_tensor(name, list(shape), dt, kind="ExternalOutput").ap()
        return nc.dram_tensor(name, list(shape), dt).ap()

    xT = din("xT", [KC, 128, NTOK])
    cck = din("cck", [L, 4, 128, 512])
    ckp = din("ckp", [L, 64, 512])
    cvec = din("cvec", [128, 32])
    ropeT = din("ropeT", [128, NSAMP])
    small = din("small", [128, NSM])
    wA = din("wA", [L, 24, 128, 2048])
    wQn = din("wQn", [L, 16, 128, 2048])
    wQr = din("wQr", [L, 16, 128, 2048])
    wKV = din("wKV", [L, 5, 128, 2048])
    wC = din("wC", [L, 16, 128, 2048])
    wG = din("wG", [L, 48, 128, 2048])
    wOA = din("wOA", [L, 16, 128, 1024])
    wO = din("wO", [L, 16, 128, 2048])
    wOC = din("wOC", [L, 16, 128, 1024])
    wM = din("wM", [L, 16, 128, 2048])
    wF1 = din("wF1", [L, 64, 128, 2048])
    wF2 = din("wF2", [L, 4, 128, 64 * 512])
    wKb = din("wKb", [L, 16, 128, 512])
    wVb = din("wVb", [L, 16, 128, 512])
    wMod = din("wMod", [L, 96, 128, 2048])

    yT = dout("yT", [KC, 128, NTOK])
    nckvT = dout("nckvT", [L, 4, 128, 512])
    nkpeT = dout("nkpeT", [L, 64, 512])

    xsA = dscr("xsA", [KC, 128, NTOK], F32)
    xsB = dscr("xsB", [KC, 128, NTOK], F32)
    yaD = dscr("yaD", [8, 128, NTOK], BF16)
    ycD = dscr("ycD", [8, 128, NTOK], BF16)
    obD = dscr("obD", [16, 128, NTOK], BF16)
    vsD = dscr("vsD", [8, 128, NTOK], F32)
    mgD = dscr("mgD", [16, 128, NTOK], BF16)
    asD = dscr("asD", [64, 128, NTOK], BF16)
    PWS = (4096, 4096, 2560)
    pay_ts = [nc.dram_tensor("pay%d" % i, [128, w], F32) for i, w in enumerate(PWS)]
    payg_ts = [nc.dram_tensor("payg%d" % i, [256, w], F32) for i, w in enumerate(PWS)]

    class _Pay:
        def __init__(self, ts):
            self.aps = [t.ap() for t in ts]

        def __getitem__(self, key):
            rows, cols = key
            c0, c1 = cols.start, cols.stop
            base = 0
            for ap_, w in zip(self.aps, PWS):
                if c0 >= base and c1 <= base + w:
                    return ap_[rows, c0 - base:c1 - base]
                base += w
            raise AssertionError(("payload slice straddles pieces", c0, c1))
    pay = _Pay(pay_ts)
    payg = _Pay(payg_ts)
    dbgD = dscr("dbgD", [128, 16 * 512], F32) if debug else None

    groups = [[2 * i, 2 * i + 1] for i in range(n_cores // 2)]

    with contextlib.ExitStack() as st:
        P = Prog(nc, st)
        SBYTES = 206 * 1024
        SB = st.enter_context(nc.sbuf_tensor("SB", [128, SBYTES // 2], BF16))
        banks = [T(st.enter_context(nc.psum_tensor("ps%d" % i, [128, 512], F32))[:]) for i in range(8)]
        bank_rr = [0]

        def nbank():
            b = banks[bank_rr[0] % 8]
            bank_rr[0] += 1
            return b

        def carve(off, n, dt):
            assert off % 4 == 0
            if dt is BF16:
                assert off + 2 * n <= SBYTES, (off, n)
                return SB[:, off // 2: off // 2 + n]
            assert off + 4 * n <= SBYTES, (off, n)
            return SB[:, off // 2: off // 2 + 2 * n].bitcast(F32)

        class Alloc:
            def __init__(self, base, limit):
                self.base, self.limit, self.p = base, limit, base

            def reset(self):
                self.p = self.base

            def get(self, n, dt):
                sz = n * (2 if dt is BF16 else 4)
                sz = (sz + 31) // 32 * 32
                a = carve(self.p, n, dt)
                self.p += sz
                assert self.p <= self.limit, ("sbuf overflow", self.p, self.limit)
                return a

        pers = Alloc(0, SBYTES)
        H_ap = pers.get(KC * NTOK, BF16)
        hT = [[T(H_ap[:, kc * NTOK + tt * TT: kc * NTOK + (tt + 1) * TT]) for tt in range(NT)] for kc in range(KC)]
        small_ap = pers.get(NSM, F32)
        small_t = T(small_ap)
        modv_ap = pers.get(L * 96 * 2, F32)
        modv_t = T(modv_ap)
        gs_ap = pers.get(L * 2 * 16 * 2, F32)
        gs_t = T(gs_ap)
        onesf_ap = pers.get(128, F32)
        onesf_t = T(onesf_ap)
        onesb_ap = pers.get(128, BF16)
        onesb_t = T(onesb_ap)
        rope_ap = pers.get(NSAMP, F32)
        rope_t = T(rope_ap)
        scb_ap = pers.get(32, BF16)
        scb_t = T(scb_ap)
        eps_ap = pers.get(8, F32)[:, 0:1]
        eps_t = T(eps_ap)
        ckvP_ap = pers.get(4 * 512, BF16)
        kpeP_ap = pers.get(512, BF16)
        ckvP_t, kpeP_t = T(ckvP_ap), T(kpeP_ap)
        PH_BASE = pers.p
        ph = Alloc(PH_BASE, SBYTES)

        def scol(l, off, i=0):
            o = (l * SL if l is not None else 0) + off + i
            return small_ap[:, o:o + 1]

        def modcol(l, idx, which):
            o = (l * 96 + idx) * 2 + which
            return modv_ap[:, o:o + 1]

        def gscol(l, nrm, kc, which):
            o = ((l * 2 + nrm) * 16 + kc) * 2 + which
            return gs_ap[:, o:o + 1]

        def wtt(tt):
            return 0 if tt < 4 else 1

        def mm_group(ps_t, ps_ap, pairs, reads):
            def fn(e, pairs=pairs, ps_ap=ps_ap):
                n = len(pairs)
                ins = None
                for i, (l_, r_) in enumerate(pairs):
                    ins = e.matmul(ps_ap, l_, r_, start=(i == 0), stop=(i == n - 1))
                return ins
            P.op("pe", fn, reads=reads, writes=[ps_t])

        evac_rr = [0]

        def evac_copy(out_ap, out_ts, ps_ap, ps_t, eng=None):
            if eng is None:
                eng = ("act", "dve")[evac_rr[0] % 2]
                evac_rr[0] += 1
            if eng == "act":
                P.op("act", lambda e: e.activation(out=out_ap, in_=ps_ap, func=AF.Copy), reads=[ps_t], writes=out_ts)
            else:
                P.op("dve", lambda e: e.tensor_copy(out=out_ap, in_=ps_ap), reads=[ps_t], writes=out_ts)

        P.dma("sp", small_ap, small[:, :], writes=[small_t])
        P.dma("sp", rope_ap, ropeT[:, :], writes=[rope_t])
        P.op("dve", lambda e: e.memset(onesf_ap, 1.0), writes=[onesf_t])
        P.op("dve", lambda e: e.memset(onesb_ap, 1.0), writes=[onesb_t])
        P.op("dve", lambda e: e.memset(eps_ap, EPS), writes=[eps_t])
        P.op("dve", lambda e: e.memset(kpeP_ap, 0.0), writes=[kpeP_t])
        cv_ap = ph.get(32, F32)
        cv_t = T(cv_ap)
        P.dma("sp", cv_ap, cvec[:, :], writes=[cv_t])
        P.op("act", lambda e: e.activation(out=scb_ap, in_=cv_ap, func=AF.Silu), reads=[cv_t], writes=[scb_t])
        z_ap = ph.get(NSAMP, F32)
        z_t = T(z_ap)
        P.op("dve", lambda e: e.memset(z_ap, 0.0), writes=[z_t])
        P.dma("sp", pay[64:128, 4 * NSAMP:5 * NSAMP], z_ap[64:128, :], reads=[z_t])

        def mod_finish(l):
            mv3 = modv_ap[:, l * 192:(l + 1) * 192].rearrange("p (m w) -> p m w", w=2)
            for nrm in range(2):
                sc_idx = 16 if nrm == 0 else 64
                g_off = O_N1G if nrm == 0 else O_N2G
                gv = small_ap[:, l * SL + g_off: l * SL + g_off + 16]
                gs3 = gs_ap[:, (l * 2 + nrm) * 32:(l * 2 + nrm + 1) * 32].rearrange("p (k w) -> p k w", w=2)
                for w_ in range(2):
                    P.op("dve", lambda e, w_=w_, gs3=gs3, gv=gv, sc_idx=sc_idx: e.scalar_tensor_tensor(
                        out=gs3[:, :, w_], in0=mv3[:, sc_idx:sc_idx + 16, w_], scalar=1.0, in1=gv, op0=ALU.add, op1=ALU.mult),
                        reads=[modv_t, small_t], writes=[gs_t])

        def mod_chunk(l, m, w):
            P.dma("pool", w.ap, wMod[l, m], writes=[w])
            ps = nbank()
            pairs = [(w.ap[:, kc * 128:(kc + 1) * 128], scb_ap[:, kc * 2:kc * 2 + 2]) for kc in range(KC)]
            mm_group(ps, ps.ap[:, 0:2], pairs, [w, scb_t])
            o = (l * 96 + m) * 2
            P.op("dve", lambda e, ps=ps, o=o, m=m: e.tensor_scalar(out=modv_ap[:, o:o + 2], in0=ps.ap[:, 0:2], scalar1=scol(l, O_BMOD, m), scalar2=None, op0=ALU.add),
                 reads=[ps, small_t], writes=[modv_t])

        def phase_mod(l):
            ph.reset()
            wr = Ring([ph.get(2048, BF16) for _ in range(4)])
            for m in range(96):
                mod_chunk(l, m, wr.next())
            mod_finish(l)
            P.barrier()

        def phase_norm(xsrc, l, final=False):
            ph.reset()
            xr = Ring([ph.get(KC * TT, F32) for _ in range(2)])
            sqr = Ring([ph.get(TT, F32) for _ in range(2)])
            sdr = Ring([ph.get(TT, F32) for _ in range(2)])
            tmr = Ring([ph.get(TT, F32) for _ in range(3)])
            yr = Ring([ph.get(4 * TT, F32) for _ in range(2)]) if final else None
            for tt in range(NT):
                w_ = wtt(tt)
                xt = xr.next()
                x3 = xt.ap.rearrange("p (k t) -> p k t", k=KC)
                P.dma_multi("sp", [(x3[:, q4 * 4:(q4 + 1) * 4, :],
                                    xsrc[q4 * 4:(q4 + 1) * 4, :, tt * TT:(tt + 1) * TT].rearrange("k p t -> p k t")) for q4 in range(4)], writes=[xt])
                ps = nbank()
                for kc in range(KC):
                    sq = sqr.next()
                    P.op("act", lambda e, sq=sq, kc=kc, x3=x3: e.activation(out=sq.ap, in_=x3[:, kc, :], func=AF.Square),
                         reads=[xt], writes=[sq])
                    P.op("pe", lambda e, sq=sq, kc=kc, ps=ps: e.matmul(ps.ap, onesf_ap, sq.ap, start=(kc == 0), stop=(kc == KC - 1)),
                         reads=[sq, onesf_t], writes=[ps])
                sd = sdr.next()
                P.op("act", lambda e, sd=sd, ps=ps: e.activation(out=sd.ap, in_=ps.ap, func=AF.Sqrt, scale=1.0 / D, bias=eps_ap),
                     reads=[ps], writes=[sd])
                P.op("dve", lambda e, sd=sd: e.reciprocal(out=sd.ap, in_=sd.ap), reads=[sd], writes=[sd])
                if final:
                    for q4 in range(4):
                        yt = yr.next()
                        y3 = yt.ap.rearrange("p (k t) -> p k t", k=4)
                        for k4 in range(4):
                            kc = q4 * 4 + k4
                            P.op("dve", lambda e, kc=kc, k4=k4, y3=y3, x3=x3, sd=sd: e.scalar_tensor_tensor(
                                out=y3[:, k4, :], in0=x3[:, kc, :], scalar=scol(None, O_FNG, kc), in1=sd.ap, op0=ALU.mult, op1=ALU.mult),
                                reads=[xt, sd, small_t], writes=[yt])
                        P.dma("sp", yT[q4 * 4:(q4 + 1) * 4, :, tt * TT:(tt + 1) * TT].rearrange("k p t -> p k t"), y3, reads=[yt])
                else:
                    for kc in range(KC):
                        tm = tmr.next()
                        P.op("dve", lambda e, kc=kc, tm=tm, x3=x3, sd=sd, w_=w_: e.scalar_tensor_tensor(
                            out=tm.ap, in0=x3[:, kc, :], scalar=gscol(l, 0, kc, w_), in1=sd.ap, op0=ALU.mult, op1=ALU.mult),
                            reads=[xt, sd, gs_t], writes=[tm])
                        P.op("act", lambda e, kc=kc, tm=tm, tt=tt, w_=w_: e.activation(
                            out=hT[kc][tt].ap, in_=tm.ap, func=AF.Identity, bias=modcol(l, 0 + kc, w_), scale=1.0),
                            reads=[tm, modv_t], writes=[hT[kc][tt]])
            P.barrier()

        def gemm_h(w, cb, ncols=128, col0=0):
            for tt in range(NT):
                ps = nbank()
                pairs = [(w.ap[:, kc * 128 + col0: kc * 128 + col0 + ncols], hT[kc][tt].ap) for kc in range(KC)]
                mm_group(ps, ps.ap[0:ncols, :], pairs, [w] + [hT[kc][tt] for kc in range(KC)])
                cb(tt, ps)

        att = {}

        def phase_halo(l):
            ph.reset()
            wr = Ring([ph.get(2048, BF16) for _ in range(4)])
            hcat_ap = ph.get(KC * 32, BF16)
            hcat_t = T(hcat_ap)
            for kc in range(KC):
                P.op("dve", lambda e, kc=kc: e.tensor_copy(out=hcat_ap[:, kc * 32:kc * 32 + 16], in_=hT[kc][0].ap[:, 0:16]),
                     reads=[hT[kc][0]], writes=[hcat_t])
                P.op("dve", lambda e, kc=kc: e.tensor_copy(out=hcat_ap[:, kc * 32 + 16:kc * 32 + 32], in_=hT[kc][3].ap[:, TT - 16:TT]),
                     reads=[hT[kc][3]], writes=[hcat_t])
            hA_ap = ph.get(256, F32)
            hC_ap = ph.get(256, F32)
            hA_t, hC_t = T(hA_ap), T(hC_ap)
            gtmp = Ring([ph.get(32, F32) for _ in range(2)])
            for (wsrc, first, second, func, dst_ap, dst_t) in ((wA, 0, 16, AF.Copy, hA_ap, hA_t), (wC, 0, 8, AF.Sigmoid, hC_ap, hC_t)):
                for j in range(8):
                    w1 = wr.next()
                    P.dma("pool", w1.ap, wsrc[l, first + j], writes=[w1])
                    w2 = wr.next()
                    P.dma("pool", w2.ap, wsrc[l, second + j], writes=[w2])
                    ps = nbank()
                    mm_group(ps, ps.ap[:, 0:32], [(w1.ap[:, kc * 128:(kc + 1) * 128], hcat_ap[:, kc * 32:(kc + 1) * 32]) for kc in range(KC)], [w1, hcat_t])
                    ps2 = nbank()
                    mm_group(ps2, ps2.ap[:, 0:32], [(w2.ap[:, kc * 128:(kc + 1) * 128], hcat_ap[:, kc * 32:(kc + 1) * 32]) for kc in range(KC)], [w2, hcat_t])
                    g = gtmp.next()
                    P.op("act", lambda e, g=g, ps2=ps2, func=func: e.activation(out=g.ap, in_=ps2.ap[:, 0:32], func=func), reads=[ps2], writes=[g])
                    P.op("dve", lambda e, g=g, ps=ps, j=j, dst_ap=dst_ap: e.tensor_tensor(out=dst_ap[:, j * 32:(j + 1) * 32], in0=ps.ap[:, 0:32], in1=g.ap, op=ALU.mult),
                         reads=[ps, g], writes=[dst_t])
            P.dma("sp", pay[:, 10240:10496], hA_ap, reads=[hA_t])
            P.dma("sp", pay[:, 10496:10752], hC_ap, reads=[hC_t])
            P.barrier()

        def phase_kv(l):
            ph.reset()
            wk_ = [T(ph.get(2048, BF16)) for _ in range(5)]
            for m in range(5):
                P.dma("pool", wk_[m].ap, wKV[l, m], writes=[wk_[m]])
            rawr = [Ring([ph.get(TT, F32) for _ in range(2)]) for _ in range(5)]
            sqr = Ring([ph.get(TT, F32) for _ in range(2)])
            sdr = Ring([ph.get(TT, F32) for _ in range(2)])
            cnr = Ring([ph.get(TT, F32) for _ in range(4)])
            t1r = Ring([ph.get(TT, F32) for _ in range(2)])
            t2r = Ring([ph.get(TT, F32) for _ in range(2)])
            for tt in range(NT):
                rd = [hT[kc][tt] for kc in range(KC)]
                raw = []
                for m in range(5):
                    ps = nbank()
                    mm_group(ps, ps.ap, [(wk_[m].ap[:, kc * 128:(kc + 1) * 128], hT[kc][tt].ap) for kc in range(KC)], [wk_[m]] + rd)
                    r = rawr[m].next()
                    evac_copy(r.ap, [r], ps.ap, ps)
                    raw.append(r)
                ps = nbank()
                for m in range(4):
                    sq = sqr.next()
                    P.op("act", lambda e, sq=sq, r=raw[m]: e.activation(out=sq.ap, in_=r.ap, func=AF.Square), reads=[raw[m]], writes=[sq])
                    P.op("pe", lambda e, sq=sq, m=m, ps=ps: e.matmul(ps.ap, onesf_ap, sq.ap, start=(m == 0), stop=(m == 3)), reads=[sq, onesf_t], writes=[ps])
                sd = sdr.next()
                P.op("act", lambda e, sd=sd, ps=ps: e.activation(out=sd.ap, in_=ps.ap, func=AF.Sqrt, scale=1.0 / 512, bias=eps_ap), reads=[ps, eps_t], writes=[sd])
                P.op("dve", lambda e, sd=sd: e.reciprocal(out=sd.ap, in_=sd.ap), reads=[sd], writes=[sd])
                for m in range(4):
                    cn = cnr.next()
                    P.op("dve", lambda e, cn=cn, m=m, r=raw[m], sd=sd: e.scalar_tensor_tensor(
                        out=cn.ap, in0=r.ap, scalar=scol(l, O_KVG, m), in1=sd.ap, op0=ALU.mult, op1=ALU.mult),
                        reads=[raw[m], sd, small_t], writes=[cn])
                    if tt < 4:
                        P.dma("sp", pay[:, m * NSAMP + tt * TT: m * NSAMP + (tt + 1) * TT], cn.ap, reads=[cn])
                    else:
                        P.dma("sp", nckvT[l, m], cn.ap, reads=[cn])
                        P.op("act", lambda e, cn=cn, m=m: e.activation(out=ckvP_ap[:, m * 512:(m + 1) * 512], in_=cn.ap, func=AF.Copy),
                             reads=[cn], writes=[ckvP_t])
                rk = raw[4]
                if tt == 4:
                    P.dma("sp", nkpeT[l], rk.ap[0:64, :], reads=[rk])
                    P.op("act", lambda e, rk=rk: e.activation(out=kpeP_ap[0:64, :], in_=rk.ap[0:64, :], func=AF.Copy), reads=[rk], writes=[kpeP_t])
                else:
                    t1, t2 = t1r.next(), t2r.next()
                    P.op("dve", lambda e, t1=t1, rk=rk, tt=tt: e.tensor_tensor(out=t1.ap[0:64, :], in0=rk.ap[0:64, :], in1=rope_ap[0:64, tt * TT:(tt + 1) * TT], op=ALU.mult),
                         reads=[rk, rope_t], writes=[t1])
                    P.op("dve", lambda e, t2=t2, rk=rk, tt=tt: e.tensor_tensor(out=t2.ap[0:64, :], in0=rk.ap[64:128, :], in1=rope_ap[64:128, tt * TT:(tt + 1) * TT], op=ALU.mult),
                         reads=[rk, rope_t], writes=[t2])
                    P.op("dve", lambda e, t1=t1, t2=t2: e.tensor_tensor(out=t1.ap[0:64, :], in0=t1.ap[0:64, :], in1=t2.ap[0:64, :], op=ALU.add),
                         reads=[t1, t2], writes=[t1])
                    P.dma("sp", pay[0:64, 4 * NSAMP + tt * TT: 4 * NSAMP + (tt + 1) * TT], t1.ap[0:64, :], reads=[t1])
            P.barrier()

        def phase_xchg(l):
            ph.reset()
            ckvS_ap = ph.get(4 * NKEY, BF16)
            kpeS_ap = ph.get(NKEY, BF16)
            halA_ap = ph.get(2 * 256, F32)
            halC_ap = ph.get(2 * 256, F32)
            att["ckvS"], att["kpeS"] = T(ckvS_ap), T(kpeS_ap)
            att["halA"], att["halC"] = T(halA_ap), T(halC_ap)
            att["base"] = ph.p
            for i in range(3):
                P.collective(lambda gp, i=i: gp.collective_compute("AllGather", ALU.bypass, replica_groups=groups,
                                                                   ins=[pay_ts[i].ap().opt()], outs=[payg_ts[i].ap().opt()]))
            P.barrier()
            for r in range(2):
                for m in range(4):
                    P.dma("pool", ckvS_ap[:, m * NKEY + r * NSAMP: m * NKEY + (r + 1) * NSAMP],
                          payg[r * 128:(r + 1) * 128, m * NSAMP:(m + 1) * NSAMP])
                P.dma("pool", kpeS_ap[0:64, r * NSAMP:(r + 1) * NSAMP], payg[r * 128:r * 128 + 64, 4 * NSAMP:5 * NSAMP])
                P.dma("sp", halA_ap[:, r * 256:(r + 1) * 256], payg[r * 128:(r + 1) * 128, 10240:10496])
                P.dma("sp", halC_ap[:, r * 256:(r + 1) * 256], payg[r * 128:(r + 1) * 128, 10496:10752])
            for m in range(4):
                P.dma("pool", ckvS_ap[:, m * NKEY + 4096: m * NKEY + 4608], cck[l, m])
            P.dma("pool", kpeS_ap[0:64, 4096:4608], ckp[l])
            P.op("dve", lambda e: e.memset(kpeS_ap[64:128, :], 0.0), writes=[att["kpeS"]])
            P.barrier()

        SEGS = ((0, NSAMP), (NSAMP, 256), (NSAMP + 256, 256))

        def seg_layout(padw):
            offs = []
            o = 0
            for (s0, ln) in SEGS:
                offs.append(o + padw)
                o += ln + 2 * padw
            return offs, o

        def pieces(tt, offs):
            out = []
            t0, t1 = tt * TT, (tt + 1) * TT
            for (s0, ln), bo in zip(SEGS, offs):
                a, b = max(t0, s0), min(t1, s0 + ln)
                if a < b:
                    out.append((a - t0, b - t0, bo + (a - s0)))
            return out

        def phase_a(l):
            ph.p = att["base"]
            offs, W = seg_layout(1)
            wr = Ring([ph.get(2048, BF16) for _ in range(3)])
            tp_ap = ph.get(W, F32)
            tp_t = T(tp_ap)
            y_ap = ph.get(W, F32)
            y_t = T(y_ap)
            gb_ap = ph.get(NTOK, F32)
            gbT = [T(gb_ap[:, tt * TT:(tt + 1) * TT]) for tt in range(NT)]
            gcr = Ring([ph.get(TT, F32) for _ in range(2)])
            oar = Ring([ph.get(NTOK, BF16) for _ in range(1)])
            P.op("dve", lambda e: e.memset(tp_ap, 0.0), writes=[tp_t])
            for j in range(8):
                wx, wb, wc = wr.next(), wr.next(), wr.next()
                P.dma("pool", wx.ap, wA[l, j], writes=[wx])
                P.dma("pool", wc.ap, wA[l, 16 + j], writes=[wc])
                P.dma("pool", wb.ap, wA[l, 8 + j], writes=[wb])
                for tt in range(NT):
                    rd = [hT[kc][tt] for kc in range(KC)]
                    psx, psc, psb = nbank(), nbank(), nbank()
                    mm_group(psx, psx.ap, [(wx.ap[:, kc * 128:(kc + 1) * 128], hT[kc][tt].ap) for kc in range(KC)], [wx] + rd)
                    mm_group(psc, psc.ap, [(wc.ap[:, kc * 128:(kc + 1) * 128], hT[kc][tt].ap) for kc in range(KC)], [wc] + rd)
                    mm_group(psb, psb.ap, [(wb.ap[:, kc * 128:(kc + 1) * 128], hT[kc][tt].ap) for kc in range(KC)], [wb] + rd)
                    gc = gcr.next()
                    P.op("act", lambda e, gc=gc, psc=psc: e.activation(out=gc.ap, in_=psc.ap, func=AF.Copy), reads=[psc], writes=[gc])
                    for (a, b, bo) in pieces(tt, offs):
                        P.op("dve", lambda e, a=a, b=b, bo=bo, psx=psx, gc=gc: e.tensor_tensor(
                            out=tp_ap[:, bo:bo + (b - a)], in0=psx.ap[:, a:b], in1=gc.ap[:, a:b], op=ALU.mult),
                            reads=[psx, gc], writes=[tp_t])
                    P.op("act", lambda e, tt=tt, psb=psb: e.activation(out=gbT[tt].ap, in_=psb.ap, func=AF.Copy), reads=[psb], writes=[gbT[tt]])
                hA = att["halA"].ap
                P.op("dve", lambda e, j=j: e.tensor_scalar(out=tp_ap[:, offs[0] - 1:offs[0]], in0=hA[:, 0 * 256 + j * 32 + 31: 0 * 256 + j * 32 + 32],
                                                            scalar1=scol(None, O_MASKL), scalar2=None, op0=ALU.mult), reads=[att["halA"], small_t], writes=[tp_t])
                P.op("dve", lambda e, j=j: e.tensor_scalar(out=tp_ap[:, offs[0] + NSAMP:offs[0] + NSAMP + 1], in0=hA[:, 1 * 256 + j * 32: 1 * 256 + j * 32 + 1],
                                                            scalar1=scol(None, O_MASKR), scalar2=None, op0=ALU.mult), reads=[att["halA"], small_t], writes=[tp_t])
                P.op("dve", lambda e, j=j: e.tensor_scalar(out=y_ap[:, 0:W - 2], in0=tp_ap[:, 0:W - 2], scalar1=scol(l, O_CAW, j * 3 + 0), scalar2=None, op0=ALU.mult),
                     reads=[tp_t, small_t], writes=[y_t])
                for k in (1, 2):
                    P.op("dve", lambda e, j=j, k=k: e.scalar_tensor_tensor(out=y_ap[:, 0:W - 2], in0=tp_ap[:, k:W - 2 + k], scalar=scol(l, O_CAW, j * 3 + k),
                                                                        in1=y_ap[:, 0:W - 2], op0=ALU.mult, op1=ALU.add), reads=[tp_t, y_t, small_t], writes=[y_t])
                oa = oar.next()
                for tt in range(NT):
                    for (a, b, bo) in pieces(tt, offs):
                        P.op("dve", lambda e, a=a, b=b, bo=bo, tt=tt, oa=oa: e.tensor_tensor(
                            out=oa.ap[:, tt * TT + a: tt * TT + b], in0=y_ap[:, bo - 1: bo - 1 + (b - a)], in1=gbT[tt].ap[:, a:b], op=ALU.mult),
                            reads=[y_t, gbT[tt]], writes=[oa])
                P.dma("sp", yaD[j], oa.ap, reads=[oa])
            P.barrier()

        def phase_c1(l):
            ph.p = att["base"]
            PADW = 15
            offs, W = seg_layout(PADW)
            WV = W - 2 * PADW
            wr = Ring([ph.get(2048, BF16) for _ in range(3)])
            up_ap = ph.get(W, F32)
            up_t = T(up_ap)
            vr = Ring([ph.get(WV, F32) for _ in range(2)])
            sgr = Ring([ph.get(TT, F32) for _ in range(2)])
            P.op("dve", lambda e: e.memset(up_ap, 0.0), writes=[up_t])
            hC = att["halC"].ap
            for j in range(8):
                wa, wg = wr.next(), wr.next()
                P.dma("pool", wa.ap, wC[l, j], writes=[wa])
                P.dma("pool", wg.ap, wC[l, 8 + j], writes=[wg])
                for tt in range(NT):
                    rd = [hT[kc][tt] for kc in range(KC)]
                    psa, psg = nbank(), nbank()
                    mm_group(psa, psa.ap, [(wa.ap[:, kc * 128:(kc + 1) * 128], hT[kc][tt].ap) for kc in range(KC)], [wa] + rd)
                    mm_group(psg, psg.ap, [(wg.ap[:, kc * 128:(kc + 1) * 128], hT[kc][tt].ap) for kc in range(KC)], [wg] + rd)
                    sg = sgr.next()
                    P.op("act", lambda e, sg=sg, psg=psg: e.activation(out=sg.ap, in_=psg.ap, func=AF.Sigmoid), reads=[psg], writes=[sg])
                    for (a, b, bo) in pieces(tt, offs):
                        P.op("dve", lambda e, a=a, b=b, bo=bo, psa=psa, sg=sg: e.tensor_tensor(
                            out=up_ap[:, bo:bo + (b - a)], in0=psa.ap[:, a:b], in1=sg.ap[:, a:b], op=ALU.mult),
                            reads=[psa, sg], writes=[up_t])
                P.op("dve", lambda e, j=j: e.tensor_scalar(out=up_ap[:, offs[0] - 15:offs[0]], in0=hC[:, j * 32 + 17: j * 32 + 32],
                                                            scalar1=scol(None, O_MASKL), scalar2=None, op0=ALU.mult), reads=[att["halC"], small_t], writes=[up_t])
                P.op("dve", lambda e, j=j: e.tensor_scalar(out=up_ap[:, offs[0] + NSAMP:offs[0] + NSAMP + 15], in0=hC[:, 256 + j * 32: 256 + j * 32 + 15],
                                                            scalar1=scol(None, O_MASKR), scalar2=None, op0=ALU.mult), reads=[att["halC"], small_t], writes=[up_t])
                v = vr.next()
                P.op("dve", lambda e, j=j, v=v: e.tensor_scalar(out=v.ap, in0=up_ap[:, 0:WV], scalar1=scol(l, O_CDW, j * 31 + 0), scalar2=scol(l, O_CDB, j),
                                                                 op0=ALU.mult, op1=ALU.add), reads=[up_t, small_t], writes=[v])
                for k in range(1, 31):
                    P.op("dve", lambda e, j=j, k=k, v=v: e.scalar_tensor_tensor(out=v.ap, in0=up_ap[:, k:k + WV], scalar=scol(l, O_CDW, j * 31 + k),
                                                                               in1=v.ap, op0=ALU.mult, op1=ALU.add), reads=[up_t, v, small_t], writes=[v])
                for (s0, ln), bo in zip(SEGS, offs):
                    P.dma("sp", vsD[j, :, s0:s0 + ln], v.ap[:, bo - PADW: bo - PADW + ln], reads=[v])
            P.barrier()

        def phase_c2(l):
            ph.p = att["base"]
            vr = Ring([ph.get(8 * TT, F32) for _ in range(1)])
            sqr = Ring([ph.get(TT, F32) for _ in range(2)])
            mnr = Ring([ph.get(TT, F32) for _ in range(2)])
            rsr = Ring([ph.get(TT, F32) for _ in range(2)])
            dr = Ring([ph.get(TT, F32) for _ in range(3)])
            orr = Ring([ph.get(8 * TT, BF16) for _ in range(1)])
            for tt in range(NT):
                vt = vr.next()
                v3 = vt.ap.rearrange("p (k t) -> p k t", k=8)
                P.dma_multi("sp", [(v3[:, q * 4:(q + 1) * 4, :], vsD[q * 4:(q + 1) * 4, :, tt * TT:(tt + 1) * TT].rearrange("k p t -> p k t")) for q in range(2)], writes=[vt])
                ps1, ps2 = nbank(), nbank()
                for j in range(8):
                    P.op("pe", lambda e, j=j, ps1=ps1, v3=v3: e.matmul(ps1.ap, onesf_ap, v3[:, j, :], start=(j == 0), stop=(j == 7)), reads=[vt, onesf_t], writes=[ps1])
                    sq = sqr.next()
                    P.op("act", lambda e, j=j, sq=sq, v3=v3: e.activation(out=sq.ap, in_=v3[:, j, :], func=AF.Square), reads=[vt], writes=[sq])
                    P.op("pe", lambda e, j=j, ps2=ps2, sq=sq: e.matmul(ps2.ap, onesf_ap, sq.ap, start=(j == 0), stop=(j == 7)), reads=[sq, onesf_t], writes=[ps2])
                mn, rs = mnr.next(), rsr.next()
                P.op("act", lambda e, mn=mn, ps1=ps1: e.activation(out=mn.ap, in_=ps1.ap, func=AF.Identity, scale=1.0 / 1024), reads=[ps1], writes=[mn])
                P.op("dve", lambda e, rs=rs, mn=mn: e.tensor_tensor(out=rs.ap, in0=mn.ap, in1=mn.ap, op=ALU.mult), reads=[mn], writes=[rs])
                P.op("dve", lambda e, rs=rs, ps2=ps2: e.scalar_tensor_tensor(out=rs.ap, in0=ps2.ap, scalar=1.0 / 1024, in1=rs.ap, op0=ALU.mult, op1=ALU.subtract),
                     reads=[ps2, rs], writes=[rs])
                P.op("act", lambda e, rs=rs: e.activation(out=rs.ap, in_=rs.ap, func=AF.Sqrt, scale=1.0, bias=eps_ap), reads=[rs], writes=[rs])
                P.op("dve", lambda e, rs=rs: e.reciprocal(out=rs.ap, in_=rs.ap), reads=[rs], writes=[rs])
                ot = orr.next()
                o3 = ot.ap.rearrange("p (k t) -> p k t", k=8)
                for j in range(8):
                    d = dr.next()
                    P.op("dve", lambda e, j=j, d=d, v3=v3, mn=mn: e.tensor_tensor(out=d.ap, in0=v3[:, j, :], in1=mn.ap, op=ALU.subtract), reads=[vt, mn], writes=[d])
                    P.op("dve", lambda e, d=d, rs=rs: e.tensor_tensor(out=d.ap, in0=d.ap, in1=rs.ap, op=ALU.mult), reads=[d, rs], writes=[d])
                    P.op("act", lambda e, j=j, d=d, o3=o3: e.activation(out=o3[:, j, :], in_=d.ap, func=AF.Silu, scale=scol(l, O_CLG, j), bias=scol(l, O_CLB, j)),
                         reads=[d, small_t], writes=[ot])
                P.dma_multi("sp", [(ycD[q * 4:(q + 1) * 4, :, tt * TT:(tt + 1) * TT].rearrange("k p t -> p k t"), o3[:, q * 4:(q + 1) * 4, :]) for q in range(2)], reads=[ot])
            P.barrier()

        def phase_b(l):
            ph.p = att["base"]
            ckvS, kpeS, ckvP, kpeP = att["ckvS"], att["kpeS"], ckvP_t, kpeP_t
            wr = Ring([ph.get(2048, BF16) for _ in range(2)])
            wkr = Ring([ph.get(512, BF16) for _ in range(2)])
            wvr = Ring([ph.get(512, BF16) for _ in range(2)])
            qn_ap = ph.get(NTOK, BF16)
            qnT = [T(qn_ap[:, tt * TT:(tt + 1) * TT]) for tt in range(NT)]
            qp_ap = ph.get(NTOK, BF16)
            qpT = [T(qp_ap[:, tt * TT:(tt + 1) * TT]) for tt in range(NT)]
            P.op("dve", lambda e: e.memset(qp_ap[64:128, :], 0.0), writes=qpT)
            kh_ap = ph.get(NKEY + 512, BF16)
            khT = [T(kh_ap[:, i * TT:(i + 1) * TT]) for i in range(10)]
            vh_ap = ph.get(NKEY + 512, BF16)
            vhT = [T(vh_ap[:, i * TT:(i + 1) * TT]) for i in range(10)]
            pr = Ring([ph.get(TT, BF16) for _ in range(5)])
            rcr = Ring([ph.get(TT, F32) for _ in range(2)])
            onr = Ring([ph.get(TT, BF16) for _ in range(2)])
            t1r = Ring([ph.get(TT, F32) for _ in range(1)])
            t2r = Ring([ph.get(TT, F32) for _ in range(1)])
            accd, accp = t1r.t[0], t2r.t[0]
            sb_, ob_, smb_ = [banks[0], banks[1], banks[2]], [banks[3], banks[4]], [banks[5]]
            gb_ = [banks[6], banks[7]]
            gi = [0]

            def gbank():
                b = gb_[gi[0] % 2]
                gi[0] += 1
                return b
            si = [0]
            oi = [0]
            for h in range(16):
                wq = wr.next()
                P.dma("pool", wq.ap, wQn[l, h], writes=[wq])
                wqr = wr.next()
                P.dma("pool", wqr.ap, wQr[l, h], writes=[wqr])
                wk = wkr.next()
                P.dma("pool", wk.ap, wKb[l, h], writes=[wk])
                wv = wvr.next()
                P.dma("pool", wv.ap, wVb[l, h], writes=[wv])
                for tt in range(NT):
                    rd = [hT[kc][tt] for kc in range(KC)]
                    ps = gbank()
                    mm_group(ps, ps.ap, [(wq.ap[:, kc * 128:(kc + 1) * 128], hT[kc][tt].ap) for kc in range(KC)], [wq] + rd)
                    evac_copy(qnT[tt].ap, [qnT[tt]], ps.ap, ps)
                    ps = gbank()
                    mm_group(ps, ps.ap, [(wqr.ap[:, kc * 128:(kc + 1) * 128], hT[kc][tt].ap) for kc in range(KC)], [wqr] + rd)
                    if tt == 4:
                        evac_copy(qpT[tt].ap[0:64, :], [qpT[tt]], ps.ap[0:64, :], ps)
                    else:
                        t1, t2 = t1r.next(), t2r.next()
                        P.op("dve", lambda e, t1=t1, ps=ps, tt=tt: e.tensor_tensor(out=t1.ap[0:64, :], in0=ps.ap[0:64, :], in1=rope_ap[0:64, tt * TT:(tt + 1) * TT], op=ALU.mult),
                             reads=[ps, rope_t], writes=[t1])
                        P.op("dve", lambda e, t2=t2, ps=ps, tt=tt: e.tensor_tensor(out=t2.ap[0:64, :], in0=ps.ap[64:128, :], in1=rope_ap[64:128, tt * TT:(tt + 1) * TT], op=ALU.mult),
                             reads=[ps, rope_t], writes=[t2])
                        P.op("dve", lambda e, t1=t1, t2=t2, tt=tt: e.tensor_tensor(out=qpT[tt].ap[0:64, :], in0=t1.ap[0:64, :], in1=t2.ap[0:64, :], op=ALU.add),
                             reads=[t1, t2], writes=[qpT[tt]])
                for kt in range(10):
                    ps = gbank()
                    if kt < 9:
                        pairs = [(wk.ap[:, kc * 128:(kc + 1) * 128], ckvS.ap[:, kc * NKEY + kt * TT: kc * NKEY + (kt + 1) * TT]) for kc in range(4)]
                        rd = [wk, ckvS]
                    else:
                        pairs = [(wk.ap[:, kc * 128:(kc + 1) * 128], ckvP.ap[:, kc * 512:(kc + 1) * 512]) for kc in range(4)]
                        rd = [wk, ckvP]
                    mm_group(ps, ps.ap, pairs, rd)
                    evac_copy(khT[kt].ap, [khT[kt]], ps.ap, ps)
                for kt in range(10):
                    ps = gbank()

                    def fn(e, kt=kt, ps=ps, wv=wv):
                        ins = None
                        for q in range(4):
                            for kc in range(4):
                                if kt < 9:
                                    lhsT = ckvS.ap[:, kc * NKEY + kt * TT + q * 128: kc * NKEY + kt * TT + (q + 1) * 128]
                                else:
                                    lhsT = ckvP.ap[:, kc * 512 + q * 128: kc * 512 + (q + 1) * 128]
                                ins = e.matmul(ps.ap[:, q * 128:(q + 1) * 128], lhsT, wv.ap[:, kc * 128:(kc + 1) * 128], start=(kc == 0), stop=(kc == 3))
                        return ins
                    P.op("pe", fn, reads=[wv, ckvS if kt < 9 else ckvP], writes=[ps])
                    evac_copy(vhT[kt].ap, [vhT[kt]], ps.ap, ps)
                jobs = []
                for qt in range(4):
                    jobs.append((qt, 0, TT, [(kc, False) for kc in range(36)]))
                for pb in range(2):
                    jobs.append((4, pb * 256, 256, [(pb * 2 + q, True) for q in range(2)]))
                for (qt, q0, qw, klist) in jobs:
                    O = ob_[oi[0] % 2]
                    SM = smb_[0]
                    oi[0] += 1
                    nk = len(klist)

                    def emit_s(i, klist=klist, qt=qt, q0=q0, qw=qw):
                        kc, is_p = klist[i]
                        S = sb_[si[0] % 3]
                        si[0] += 1
                        if is_p:
                            lk = khT[9].ap[:, kc * 128:(kc + 1) * 128]
                            lp = kpeP.ap[:, kc * 128:(kc + 1) * 128]
                            rd = [khT[9], kpeP]
                        else:
                            lk = khT[kc // 4].ap[:, (kc % 4) * 128:(kc % 4 + 1) * 128]
                            lp = kpeS.ap[:, kc * 128:(kc + 1) * 128]
                            rd = [khT[kc // 4], kpeS]
                        pairs = [(lk, qnT[qt].ap[:, q0:q0 + qw]), (lp, qpT[qt].ap[:, q0:q0 + qw])]
                        mm_group(S, S.ap[:, 0:qw], pairs, rd + [qnT[qt], qpT[qt]])
                        p = pr.next()
                        P.op("act", lambda e, p=p, S=S, qw=qw: e.activation(out=p.ap[:, 0:qw], in_=S.ap[:, 0:qw], func=AF.Exp, scale=ATTN_SCALE), reads=[S], writes=[p])
                        return p

                    def emit_o(i, p, klist=klist, O=O, SM=SM, qw=qw, nk=nk):
                        kc, is_p = klist[i]
                        if is_p:
                            lv = vhT[9].ap[:, kc * 128:(kc + 1) * 128]
                            rd = [vhT[9]]
                        else:
                            lv = vhT[kc // 4].ap[:, (kc % 4) * 128:(kc % 4 + 1) * 128]
                            rd = [vhT[kc // 4]]

                        def fn(e, lv=lv, p=p, i=i, O=O, qw=qw, nk=nk):
                            return e.matmul(O.ap[:, 0:qw], lv, p.ap[:, 0:qw], start=(i == 0), stop=(i == nk - 1))
                        P.op("pe", fn, reads=rd + [p], writes=[O])
                        eng_, acc = ("dve", accd) if i % 2 == 0 else ("pool", accp)
                        if i < 2:
                            P.op(eng_, lambda e, p=p, acc=acc, qw=qw: e.tensor_copy(out=acc.ap[:, 0:qw], in_=p.ap[:, 0:qw]), reads=[p], writes=[acc])
                        else:
                            P.op(eng_, lambda e, p=p, acc=acc, qw=qw: e.tensor_tensor(out=acc.ap[:, 0:qw], in0=acc.ap[:, 0:qw], in1=p.ap[:, 0:qw], op=ALU.add),
                                 reads=[p, acc], writes=[acc])
                    pend = []
                    for i in range(nk):
                        pend.append((i, emit_s(i)))
                        if len(pend) > 2:
                            i0, p0 = pend.pop(0)
                            emit_o(i0, p0)
                    for (i0, p0) in pend:
                        emit_o(i0, p0)
                    P.op("pe", lambda e, SM=SM, qw=qw: e.matmul(SM.ap[:, 0:qw], onesf_ap, accd.ap[:, 0:qw], start=True, stop=False), reads=[accd, onesf_t], writes=[SM])
                    P.op("pe", lambda e, SM=SM, qw=qw: e.matmul(SM.ap[:, 0:qw], onesf_ap, accp.ap[:, 0:qw], start=False, stop=True), reads=[accp, onesf_t], writes=[SM])
                    rc = rcr.next()
                    P.op("dve", lambda e, rc=rc, SM=SM, qw=qw: e.reciprocal(out=rc.ap[:, 0:qw], in_=SM.ap[:, 0:qw]), reads=[SM], writes=[rc])
                    on = onr.next()
                    P.op("dve", lambda e, rc=rc, on=on, O=O, qw=qw: e.tensor_tensor(out=on.ap[:, 0:qw], in0=O.ap[:, 0:qw], in1=rc.ap[:, 0:qw], op=ALU.mult), reads=[O, rc], writes=[on])
                    P.dma("sp", obD[h, :, qt * TT + q0: qt * TT + q0 + qw], on.ap[:, 0:qw], reads=[on])
            P.barrier()

        def phase_m1(l):
            ph.reset()
            wgr = Ring([ph.get(2048, BF16) for _ in range(6)])
            war = Ring([ph.get(1024, BF16) for _ in range(2)])
            wor = Ring([ph.get(2048, BF16) for _ in range(2)])
            wcr = Ring([ph.get(1024, BF16) for _ in range(2)])
            ya_ap = ph.get(8 * TT, BF16)
            ob_ap = ph.get(16 * TT, BF16)
            yc_ap = ph.get(8 * TT, BF16)
            ya_t, ob_t, yc_t = T(ya_ap), T(ob_ap), T(yc_ap)
            sgr = Ring([ph.get(TT, F32) for _ in range(4)])
            acr = Ring([ph.get(TT, F32) for _ in range(3)])
            mgr = Ring([ph.get(TT, BF16) for _ in range(3)])
            for tt in range(NT):
                tsl = slice(tt * TT, (tt + 1) * TT)
                for (dst, t_, src, n) in ((ya_ap, ya_t, yaD, 8), (ob_ap, ob_t, obD, 16), (yc_ap, yc_t, ycD, 8)):
                    d3 = dst.rearrange("p (k t) -> p k t", k=n)
                    P.dma_multi("sp", [(d3[:, q * 4:(q + 1) * 4, :], src[q * 4:(q + 1) * 4, :, tsl].rearrange("k p t -> p k t")) for q in range(n // 4)], writes=[t_])
                rdh = [hT[kc][tt] for kc in range(KC)]
                for m in range(16):
                    wg = [wgr.next() for _ in range(3)]
                    for b_ in range(3):
                        P.dma("pool", wg[b_].ap, wG[l, b_ * 16 + m], writes=[wg[b_]])
                    wa, wo, wc = war.next(), wor.next(), wcr.next()
                    P.dma("pool", wa.ap, wOA[l, m], writes=[wa])
                    P.dma("pool", wo.ap, wO[l, m], writes=[wo])
                    P.dma("pool", wc.ap, wOC[l, m], writes=[wc])
                    psg = [nbank() for _ in range(3)]
                    for b_ in range(3):
                        mm_group(psg[b_], psg[b_].ap, [(wg[b_].ap[:, kc * 128:(kc + 1) * 128], hT[kc][tt].ap) for kc in range(KC)], [wg[b_]] + rdh)
                    psy = [nbank() for _ in range(3)]
                    mm_group(psy[0], psy[0].ap, [(wa.ap[:, kc * 128:(kc + 1) * 128], ya_ap[:, kc * TT:(kc + 1) * TT]) for kc in range(8)], [wa, ya_t])
                    mm_group(psy[1], psy[1].ap, [(wo.ap[:, kc * 128:(kc + 1) * 128], ob_ap[:, kc * TT:(kc + 1) * TT]) for kc in range(16)], [wo, ob_t])
                    mm_group(psy[2], psy[2].ap, [(wc.ap[:, kc * 128:(kc + 1) * 128], yc_ap[:, kc * TT:(kc + 1) * TT]) for kc in range(8)], [wc, yc_t])
                    sg = [sgr.next() for _ in range(3)]
                    for b_ in range(3):
                        P.op("act", lambda e, b_=b_, sg=sg, psg=psg: e.activation(out=sg[b_].ap, in_=psg[b_].ap, func=AF.Sigmoid), reads=[psg[b_]], writes=[sg[b_]])
                    ac = acr.next()
                    P.op("dve", lambda e, ac=ac, psy=psy, sg=sg: e.tensor_tensor(out=ac.ap, in0=psy[0].ap, in1=sg[0].ap, op=ALU.mult), reads=[psy[0], sg[0]], writes=[ac])
                    P.op("dve", lambda e, psy=psy, sg=sg: e.tensor_tensor(out=sg[1].ap, in0=psy[1].ap, in1=sg[1].ap, op=ALU.mult), reads=[psy[1], sg[1]], writes=[sg[1]])
                    P.op("dve", lambda e, psy=psy, sg=sg: e.tensor_tensor(out=sg[2].ap, in0=psy[2].ap, in1=sg[2].ap, op=ALU.mult), reads=[psy[2], sg[2]], writes=[sg[2]])
                    P.op("dve", lambda e, ac=ac, sg=sg: e.tensor_tensor(out=ac.ap, in0=ac.ap, in1=sg[1].ap, op=ALU.add), reads=[ac, sg[1]], writes=[ac])
                    mg = mgr.next()
                    P.op("dve", lambda e, ac=ac, sg=sg, mg=mg: e.tensor_tensor(out=mg.ap, in0=ac.ap, in1=sg[2].ap, op=ALU.add), reads=[ac, sg[2]], writes=[mg])
                    P.dma("sp", mgD[m, :, tsl], mg.ap, reads=[mg])
            P.barrier()

        def phase_m2(l, xsrc, xdst):
            ph.reset()
            wr = Ring([ph.get(2048, BF16) for _ in range(3)])
            mr = Ring([ph.get(KC * TT, BF16) for _ in range(2)])
            xr = Ring([ph.get(KC * TT, F32) for _ in range(1)])
            sqr = Ring([ph.get(TT, F32) for _ in range(3)])
            sdr = Ring([ph.get(TT, F32) for _ in range(2)])
            tmr = Ring([ph.get(TT, F32) for _ in range(3)])
            for tt in range(NT):
                w_ = wtt(tt)
                tsl = slice(tt * TT, (tt + 1) * TT)
                mt, xt = mr.next(), xr.next()
                m3 = mt.ap.rearrange("p (k t) -> p k t", k=KC)
                x3 = xt.ap.rearrange("p (k t) -> p k t", k=KC)
                P.dma_multi("sp", [(m3[:, q * 4:(q + 1) * 4, :], mgD[q * 4:(q + 1) * 4, :, tsl].rearrange("k p t -> p k t")) for q in range(4)], writes=[mt])
                P.dma_multi("sp", [(x3[:, q * 4:(q + 1) * 4, :], xsrc[q * 4:(q + 1) * 4, :, tsl].rearrange("k p t -> p k t")) for q in range(4)], writes=[xt])
                pss = banks[7]
                pend_st = []
                for m in range(16):
                    w = wr.next()
                    P.dma("pool", w.ap, wM[l, m], writes=[w])
                    ps = banks[m % 7]
                    mm_group(ps, ps.ap, [(w.ap[:, kc * 128:(kc + 1) * 128], m3[:, kc, :]) for kc in range(KC)], [w, mt])
                    P.op("dve", lambda e, m=m, ps=ps, x3=x3, w_=w_: e.scalar_tensor_tensor(out=x3[:, m, :], in0=ps.ap, scalar=modcol(l, 32 + m, w_), in1=x3[:, m, :],
                                                                                      op0=ALU.mult, op1=ALU.add), reads=[ps, xt, modv_t], writes=[xt])
                    sq = sqr.next()
                    P.op("act", lambda e, m=m, sq=sq, x3=x3: e.activation(out=sq.ap, in_=x3[:, m, :], func=AF.Square), reads=[xt], writes=[sq])
                    pend_st.append((m, sq))
                    if len(pend_st) > 1:
                        m0, sq0 = pend_st.pop(0)
                        P.op("pe", lambda e, m0=m0, sq0=sq0, pss=pss: e.matmul(pss.ap, onesf_ap, sq0.ap, start=(m0 == 0), stop=(m0 == 15)), reads=[sq0, onesf_t], writes=[pss])
                for (m0, sq0) in pend_st:
                    P.op("pe", lambda e, m0=m0, sq0=sq0, pss=pss: e.matmul(pss.ap, onesf_ap, sq0.ap, start=(m0 == 0), stop=(m0 == 15)), reads=[sq0, onesf_t], writes=[pss])
                P.dma_multi("sp", [(xdst[q * 4:(q + 1) * 4, :, tsl].rearrange("k p t -> p k t"), x3[:, q * 4:(q + 1) * 4, :]) for q in range(4)], reads=[xt])
                sd = sdr.next()
                P.op("act", lambda e, sd=sd, pss=pss: e.activation(out=sd.ap, in_=pss.ap, func=AF.Sqrt, scale=1.0 / D, bias=eps_ap), reads=[pss], writes=[sd])
                P.op("dve", lambda e, sd=sd: e.reciprocal(out=sd.ap, in_=sd.ap), reads=[sd], writes=[sd])
                for kc in range(KC):
                    tm = tmr.next()
                    P.op("dve", lambda e, kc=kc, tm=tm, x3=x3, sd=sd, w_=w_: e.scalar_tensor_tensor(
                        out=tm.ap, in0=x3[:, kc, :], scalar=gscol(l, 1, kc, w_), in1=sd.ap, op0=ALU.mult, op1=ALU.mult), reads=[xt, sd, gs_t], writes=[tm])
                    P.op("act", lambda e, kc=kc, tm=tm, tt=tt, w_=w_: e.activation(out=hT[kc][tt].ap, in_=tm.ap, func=AF.Identity, bias=modcol(l, 48 + kc, w_), scale=1.0),
                         reads=[tm, modv_t], writes=[hT[kc][tt]])
            P.barrier()

        def phase_f1(l, mod_next=None):
            ph.reset()
            wr = Ring([ph.get(2048, BF16) for _ in range(4)])
            wmr = Ring([ph.get(2048, BF16) for _ in range(3)])
            mod_i = [0]
            rr_ = Ring([ph.get(TT, F32) for _ in range(3)])
            ar = Ring([ph.get(NTOK, BF16) for _ in range(3)])
            for f in range(64):
                w = wr.next()
                P.dma("pool", w.ap, wF1[l, f], writes=[w])
                at = ar.next()
                for tt in range(NT):
                    ps = nbank()
                    mm_group(ps, ps.ap, [(w.ap[:, kc * 128:(kc + 1) * 128], hT[kc][tt].ap) for kc in range(KC)], [w] + [hT[kc][tt] for kc in range(KC)])
                    r = rr_.next()
                    P.op("act", lambda e, r=r, ps=ps: e.activation(out=r.ap, in_=ps.ap, func=AF.Relu), reads=[ps], writes=[r])
                    P.op("dve", lambda e, r=r, at=at, tt=tt: e.tensor_tensor(out=at.ap[:, tt * TT:(tt + 1) * TT], in0=r.ap, in1=r.ap, op=ALU.mult), reads=[r], writes=[at])
                P.dma("sp", asD[f], at.ap, reads=[at])
                if mod_next is not None:
                    while mod_i[0] < 96 and mod_i[0] < (f + 1) * 1.5:
                        mod_chunk(mod_next, mod_i[0], wmr.next())
                        mod_i[0] += 1
            if mod_next is not None:
                mod_finish(mod_next)
            P.barrier()

        def phase_f2(l, xsrc, xdst):
            ph.reset()
            hreg = Alloc(0, KC * NTOK * 2)
            wr = [[T(hreg.get(16 * 512, BF16)) for _ in range(4)], [T(ph.get(16 * 512, BF16)) for _ in range(4)]]
            ar = Ring([ph.get(8 * TT, BF16) for _ in range(3)])
            xr = Ring([ph.get(TT, F32) for _ in range(8)])

            def load_w(g):
                for q in range(4):
                    P.dma("pool", wr[g % 2][q].ap, wF2[l, g, :, q * 16 * 512:(q + 1) * 16 * 512], writes=[wr[g % 2][q]])
            load_w(0)
            for g in range(4):
                wq = wr[g % 2]
                if g + 1 < 4:
                    load_w(g + 1)
                for tt in range(NT):
                    tsl = slice(tt * TT, (tt + 1) * TT)
                    xts = []
                    for mi in range(4):
                        xt = xr.next()
                        P.dma("sp", xt.ap, xsrc[g * 4 + mi, :, tsl], writes=[xt])
                        xts.append(xt)
                    pb = [nbank() for _ in range(4)]
                    for blk in range(8):
                        at = ar.next()
                        a3 = at.ap.rearrange("p (k t) -> p k t", k=8)
                        P.dma_multi("sp", [(a3[:, q * 4:(q + 1) * 4, :], asD[blk * 8 + q * 4: blk * 8 + (q + 1) * 4, :, tsl].rearrange("k p t -> p k t")) for q in range(2)], writes=[at])
                        wt = wq[blk // 2]

                        def fn(e, blk=blk, a3=a3, wt=wt, pb=pb):
                            ins = None
                            for k8 in range(8):
                                kc = blk * 8 + k8
                                kl = kc % 16
                                for mi in range(4):
                                    ins = e.matmul(pb[mi].ap, wt.ap[:, kl * 512 + mi * 128: kl * 512 + (mi + 1) * 128], a3[:, k8, :], start=(kc == 0), stop=(kc == 63))
                            return ins
                        P.op("pe", fn, reads=[at, wt], writes=pb)
                    for mi in range(4):
                        m = g * 4 + mi
                        xt = xts[mi]
                        P.op("dve", lambda e, m=m, mi=mi, xt=xt, pb=pb, tt=tt: e.scalar_tensor_tensor(out=xt.ap, in0=pb[mi].ap, scalar=modcol(l, 80 + m, wtt(tt)), in1=xt.ap,
                                                                                           op0=ALU.mult, op1=ALU.add), reads=[pb[mi], xt, modv_t], writes=[xt])
                        P.dma("act", xdst[m, :, tsl], xt.ap, reads=[xt])
            P.barrier()

        P.barrier()
        xcur = xT
        for l in range(L):
            plist = [(lambda: phase_mod(l)) if l == 0 else (lambda: None), lambda: phase_norm(xcur, l), lambda: phase_halo(l), lambda: phase_kv(l), lambda: phase_xchg(l),
                     lambda: phase_a(l), lambda: phase_c1(l), lambda: phase_c2(l), lambda: phase_b(l), lambda: phase_m1(l),
                     lambda: phase_m2(l, xcur, xsA), lambda: phase_f1(l, l + 1 if l + 1 < L else None), lambda: phase_f2(l, xsA, xsB)]
            for i, f in enumerate(plist):
                if i < stop:
                    f()
            xcur = xsB
        if stop >= 99:
            phase_norm(xcur, None, final=True)
        P.barrier()
        with nc.Block() as block:
            P.emit(block)
    return nc


def _lay(W):
    K, N = W.shape
    return np.ascontiguousarray(W.reshape(K // 128, 128, N // 128, 128).transpose(2, 1, 0, 3)).reshape(N // 128, 128, K)


_X1 = [a * 32 + f for a in range(2) for f in range(16)]
_X2 = [a * 32 + 16 + f for a in range(2) for f in range(16)]
_ROPE_COLS = _X1 + _X2 + _X2 + _X1
_PERM = _X1 + _X2


def prep_shared(inp, L=DEPTH):
    w_in = np.asarray(inp["w_in"], np.float32)
    sh = {}
    sh["wA"] = np.stack([_lay(w_in[l][:, 0:3072]) for l in range(L)])
    qn_cols = np.concatenate([OFF_Q + h * 192 + np.arange(128) for h in range(16)])
    qr_cols = np.concatenate([OFF_Q + h * 192 + 128 + np.array(_ROPE_COLS) for h in range(16)])
    sh["wQn"] = np.stack([_lay(w_in[l][:, qn_cols]) for l in range(L)])
    sh["wQr"] = np.stack([_lay(w_in[l][:, qr_cols]) for l in range(L)])
    kv_cols = np.concatenate([OFF_KV + np.arange(512), OFF_KV + 512 + np.array(_ROPE_COLS)])
    sh["wKV"] = np.stack([_lay(w_in[l][:, kv_cols]) for l in range(L)])
    sh["wC"] = np.stack([_lay(w_in[l][:, OFF_CONF:OFF_CONF + 2048]) for l in range(L)])
    sh["wG"] = np.stack([_lay(w_in[l][:, OFF_GATE:OFF_GATE + 6144]) for l in range(L)])
    sh["wOA"] = np.stack([_lay(np.asarray(inp["w_out_a"][l], np.float32)) for l in range(L)])
    sh["wO"] = np.stack([_lay(np.asarray(inp["w_o_attn"][l], np.float32).reshape(2048, 2048)) for l in range(L)])
    sh["wOC"] = np.stack([_lay(np.asarray(inp["w_out_c"][l], np.float32)) for l in range(L)])
    sh["wM"] = np.stack([_lay(np.asarray(inp["w_merge"][l], np.float32)) for l in range(L)])
    sh["wF1"] = np.stack([_lay(np.asarray(inp["w_ff1"][l], np.float32)) for l in range(L)])
    sh["wF2"] = np.stack([np.ascontiguousarray(np.asarray(inp["w_ff2"][l], np.float32).reshape(64, 128, 4, 512).transpose(2, 1, 0, 3)).reshape(4, 128, 64 * 512)
                          for l in range(L)])
    wkvb = np.asarray(inp["w_kv_b"], np.float32)
    sh["wKb"] = np.stack([np.concatenate([_lay(wkvb[l][:, h, 0:128]) for h in range(16)]) for l in range(L)])
    sh["wVb"] = np.stack([np.concatenate([_lay(wkvb[l][:, h, 128:256]) for h in range(16)]) for l in range(L)])
    sh["wMod"] = np.stack([_lay(np.asarray(inp["w_mod"][l], np.float32)) for l in range(L)])
    return sh


def _fm(v):
    v = np.asarray(v, np.float32)
    return np.ascontiguousarray(v.reshape(-1, 128).T)


def prep_core(inp, c, L=DEPTH):
    b, half = c // 2, c % 2
    xs = np.asarray(inp["x_sample"][b, half * NSAMP:(half + 1) * NSAMP], np.float32)
    xp = np.asarray(inp["x_prompt"][2 * c:2 * c + 2], np.float32).reshape(512, D)
    xtok = np.concatenate([xs, xp], 0)
    d = {}
    d["xT"] = np.ascontiguousarray(xtok.T).reshape(KC, 128, NTOK)
    cc = np.asarray(inp["cache_ckv"][b, :L], np.float32)
    d["cck"] = np.ascontiguousarray(cc.transpose(0, 2, 1)).reshape(L, 4, 128, 512)
    ck = np.asarray(inp["cache_kpe"][b, :L], np.float32)
    d["ckp"] = np.ascontiguousarray(ck[:, :, _PERM].transpose(0, 2, 1))
    cv = np.stack([np.asarray(inp["c"][b], np.float32), np.asarray(inp["c_ctx"], np.float32)], -1)
    d["cvec"] = np.ascontiguousarray(cv.reshape(KC, 128, 2).transpose(1, 0, 2)).reshape(128, 32)
    t = np.arange(NSAMP, dtype=np.int64) + half * NSAMP
    pos = np.stack([(t // 64).astype(np.float32), (t % 64).astype(np.float32)], 0)
    inv = (np.float32(10000.0) ** (-np.arange(16, dtype=np.float32) / np.float32(16))).astype(np.float32)
    ang = (pos[:, None, :] * inv[None, :, None]).astype(np.float32)
    cs, sn = np.cos(ang).reshape(32, NSAMP), np.sin(ang).reshape(32, NSAMP)
    d["ropeT"] = np.ascontiguousarray(np.concatenate([cs, cs, -sn, sn], 0).astype(np.float32))
    sm = np.zeros((128, NSM), np.float32)
    for l in range(L):
        o = l * SL
        sm[:, o + O_N1G:o + O_N1G + 16] = _fm(inp["norm1_g"][l])
        sm[:, o + O_N2G:o + O_N2G + 16] = _fm(inp["norm2_g"][l])
        sm[:, o + O_BMOD:o + O_BMOD + 96] = _fm(inp["b_mod"][l])
        sm[:, o + O_KVG:o + O_KVG + 4] = _fm(inp["kv_norm_g"][l])
        caw = np.asarray(inp["conv_a_w"][l], np.float32)
        sm[:, o + O_CAW:o + O_CAW + 24] = caw.reshape(3, 8, 128).transpose(2, 1, 0).reshape(128, 24)
        cdw = np.asarray(inp["conf_dw_w"][l], np.float32)
        sm[:, o + O_CDW:o + O_CDW + 248] = cdw.reshape(31, 8, 128).transpose(2, 1, 0).reshape(128, 248)
        sm[:, o + O_CDB:o + O_CDB + 8] = _fm(inp["conf_dw_b"][l])
        sm[:, o + O_CLG:o + O_CLG + 8] = _fm(inp["conf_ln_g"][l])
        sm[:, o + O_CLB:o + O_CLB + 8] = _fm(inp["conf_ln_b"][l])
    sm[:, O_FNG:O_FNG + 16] = _fm(inp["final_norm_g"])
    sm[:, O_MASKL] = 1.0 if half == 1 else 0.0
    sm[:, O_MASKR] = 1.0 if half == 0 else 0.0
    d["small"] = sm
    return d


def assemble(results, n_cores=8, L=DEPTH):
    nb_p, nb_s = 2 * n_cores, n_cores // 2
    y_prompt = np.zeros((nb_p, 256, D), np.float32)
    y_sample = np.zeros((nb_s, 4096, D), np.float32)
    new_ckv = np.zeros((nb_p, L, 256, 512), np.float32)
    new_kpe = np.zeros((nb_p, L, 256, 64), np.float32)
    inv_perm = np.argsort(np.array(_PERM))
    for c in range(n_cores):
        r = results[c]
        b, half = c // 2, c % 2
        ytok = np.asarray(r["yT"]).reshape(D, NTOK).T
        y_sample[b, half * NSAMP:(half + 1) * NSAMP] = ytok[0:NSAMP]
        y_prompt[2 * c] = ytok[NSAMP:NSAMP + 256]
        y_prompt[2 * c + 1] = ytok[NSAMP + 256:]
        ck = np.asarray(r["nckvT"]).reshape(L, 512, 512)
        kp = np.asarray(r["nkpeT"])
        for i in range(2):
            new_ckv[2 * c + i] = ck[:, :, i * 256:(i + 1) * 256].transpose(0, 2, 1)
            new_kpe[2 * c + i] = kp[:, inv_perm, i * 256:(i + 1) * 256].transpose(0, 2, 1)
    return y_prompt, y_sample, new_ckv, new_kpe


_NC_CACHE = {}


def kernel(**inputs):
    L, n_cores = DEPTH, 8
    key = (L, n_cores)
    if key not in _NC_CACHE:
        _NC_CACHE[key] = build_program(L, n_cores)
    nc = _NC_CACHE[key]
    sh = prep_shared(inputs, L)
    in_maps = []
    for c in range(n_cores):
        d = prep_core(inputs, c, L)
        d.update(sh)
        in_maps.append(d)
    res = run_bass_kernel_spmd(nc, in_maps, core_ids=list(range(n_cores)))
    return assemble(res.results, n_cores, L)
```

```python
import contextlib
import numpy as np
import concourse.bass as bass
import concourse.mybir as mybir
from concourse.bass_utils import run_bass_kernel_spmd

F32 = mybir.dt.float32
BF16 = mybir.dt.bfloat16
AF = mybir.ActivationFunctionType
ALU = mybir.AluOpType

D = 2048
KC = 16
NTOK = 2560
NSAMP = 2048
TT = 512
NT = 5
DEPTH = 2
OFF_Q = 3072
OFF_KV = 6144
OFF_CONF = 6720
OFF_GATE = 8768
EPS = 1e-6
ATTN_SCALE = 192 ** -0.5
NKEY = 4608
PW = 10240 + 512
SL = 428
NSM = 2 * SL + 18
O_N1G, O_N2G, O_BMOD, O_KVG, O_CAW, O_CDW, O_CDB, O_CLG, O_CLB = 0, 16, 32, 128, 132, 156, 404, 412, 420
O_FNG, O_MASKL, O_MASKR = 2 * SL, 2 * SL + 16, 2 * SL + 17

ENG = ("pe", "act", "dve", "pool", "sp")
SAME_ENGINE_SYNC = ("act", "dve", "pool")


class T:
    __slots__ = ("ap", "lw", "rd")

    def __init__(self, ap=None):
        self.ap = ap
        self.lw = []
        self.rd = {}


class Prog:
    def __init__(self, nc, stack, n_dma_sems=(("sp", 28), ("pool", 28), ("act", 8))):
        self.nc = nc
        self.ops = {e: [] for e in ENG}
        self.tick = {e: 0 for e in ENG}
        self.known = {e: {} for e in ENG}
        self.sem = {}
        for e in ENG:
            self.sem[e] = stack.enter_context(nc.semaphore("s_" + e))
        self.slots = {}
        self.cnt = {}
        self.amt = {}
        self.rr = {}
        for q, n in n_dma_sems:
            self.slots[q] = []
            for i in range(n):
                key = "d_%s_%d" % (q, i)
                self.sem[key] = stack.enter_context(nc.semaphore(key))
                self.slots[q].append(key)
                self.cnt[key] = 0
                self.amt[key] = 16
            self.rr[q] = 0
        self.sem["cc"] = stack.enter_context(nc.semaphore("cc"))
        self.cnt["cc"] = 0
        self.amt["cc"] = 1

    def _need(self, e, deps):
        kn = self.known[e]
        best = {}
        for (k, v) in deps:
            if k == e and e not in SAME_ENGINE_SYNC:
                continue
            if kn.get(k, 0) >= v:
                continue
            if best.get(k, 0) < v:
                best[k] = v
        for k, v in best.items():
            kn[k] = v
        return list(best.items())

    @staticmethod
    def _deps(reads, writes):
        deps = []
        for t in reads:
            deps.extend(t.lw)
        for t in writes:
            deps.extend(t.lw)
            deps.extend(t.rd.items())
        return deps

    @staticmethod
    def _mark(ev, reads, writes):
        k, v = ev
        for t in reads:
            if t.rd.get(k, 0) < v:
                t.rd[k] = v
        for t in writes:
            t.lw = [ev]
            t.rd = {}

    def op(self, e, fn, reads=(), writes=()):
        waits = self._need(e, self._deps(reads, writes))
        self.tick[e] += 1
        ev = (e, self.tick[e])
        self.ops[e].append((waits, fn, (e, 1)))
        self._mark(ev, reads, writes)
        return ev

    def dma(self, q, out_ap, in_ap, reads=(), writes=()):
        slots = self.slots[q]
        key = slots[self.rr[q] % len(slots)]
        self.rr[q] += 1
        deps = self._deps(reads, writes)
        if self.cnt[key] > 0:
            deps.append((key, 16 * self.cnt[key]))
        waits = self._need(q, deps)
        self.cnt[key] += 1
        ev = (key, 16 * self.cnt[key])

        def fn(eng, out_ap=out_ap, in_ap=in_ap):
            return eng.dma_start(out=out_ap, in_=in_ap)
        self.ops[q].append((waits, fn, (key, 16)))
        self._mark(ev, reads, writes)
        return ev

    def dma_multi(self, q, pairs, reads=(), writes=()):
        deps = self._deps(reads, writes)
        evs = []
        for (out_ap, in_ap) in pairs:
            slots = self.slots[q]
            key = slots[self.rr[q] % len(slots)]
            self.rr[q] += 1
            d = list(deps)
            if self.cnt[key] > 0:
                d.append((key, 16 * self.cnt[key]))
            waits = self._need(q, d)
            self.cnt[key] += 1
            ev = (key, 16 * self.cnt[key])
            evs.append(ev)

            def fn(eng, out_ap=out_ap, in_ap=in_ap):
                return eng.dma_start(out=out_ap, in_=in_ap)
            self.ops[q].append((waits, fn, (key, 16)))
        for ev in evs:
            k, v = ev
            for t in reads:
                if t.rd.get(k, 0) < v:
                    t.rd[k] = v
        for t in writes:
            t.lw = list(evs)
            t.rd = {}
        return evs

    def collective(self, fn):
        waits = self._need("pool", [("cc", self.cnt["cc"])]) if self.cnt["cc"] > 0 else []
        self.cnt["cc"] += 1
        self.ops["pool"].append((waits, fn, ("cc", 1)))

    def barrier(self):
        evs = [(e, self.tick[e]) for e in ENG if self.tick[e] > 0]
        for key, c in self.cnt.items():
            if c > 0:
                evs.append((key, self.amt[key] * c))
        for e in ENG:
            waits = self._need(e, list(evs))
            if waits:
                self.ops[e].append((waits, None, None))

    def emit(self, block):
        prog = self

        def run(e, eng):
            for (waits, fn, inc) in prog.ops[e]:
                for (k, v) in waits:
                    eng.wait_ge(prog.sem[k], v)
                if fn is None:
                    continue
                ins = fn(eng)
                if inc is not None:
                    ins.then_inc(prog.sem[inc[0]], inc[1])

        @block.tensor
        def _(eng):
            run("pe", eng)

        @block.scalar
        def _(eng):
            run("act", eng)

        @block.vector
        def _(eng):
            run("dve", eng)

        @block.gpsimd
        def _(eng):
            run("pool", eng)

        @block.sync
        def _(eng):
            run("sp", eng)


class Ring:
    def __init__(self, aps):
        self.t = [T(a) for a in aps]
        self.i = 0

    def next(self):
        t = self.t[self.i % len(self.t)]
        self.i += 1
        return t


def build_program(L=DEPTH, n_cores=8, debug=False, stop=99):
    nc = bass.Bass("TRN2", target_bir_lowering=False)

    def din(name, shape, dt=F32):
        return nc.dram_tensor(name, list(shape), dt, kind="ExternalInput").ap()

    def dout(name, shape, dt=F32):
        return nc.dram_tensor(name, list(shape), dt, kind="ExternalOutput").ap()

    def dscr(name, shape, dt):
        if debug:
            return nc.dram_tensor(name, list(shape), dt, kind="ExternalOutput").ap()
        return nc.dram_tensor(name, list(shape), dt).ap()

    xT = din("xT", [KC, 128, NTOK])
    cck = din("cck", [L, 4, 128, 512])
    ckp = din("ckp", [L, 64, 512])
    cvec = din("cvec", [128, 32])
    ropeT = din("ropeT", [128, NSAMP])
    small = din("small", [128, NSM])
    wA = din("wA", [L, 24, 128, 2048])
    wQn = din("wQn", [L, 16, 128, 2048])
    wQr = din("wQr", [L, 16, 128, 2048])
    wKV = din("wKV", [L, 5, 128, 2048])
    wC = din("wC", [L, 16, 128, 2048])
    wG = din("wG", [L, 48, 128, 2048])
    wOA = din("wOA", [L, 16, 128, 1024])
    wO = din("wO", [L, 16, 128, 2048])
    wOC = din("wOC", [L, 16, 128, 1024])
    wM = din("wM", [L, 16, 128, 2048])
    wF1 = din("wF1", [L, 64, 128, 2048])
    wF2 = din("wF2", [L, 4, 128, 64 * 512])
    wKb = din("wKb", [L, 16, 128, 512])
    wVb = din("wVb", [L, 16, 128, 512])
    wMod = din("wMod", [L, 96, 128, 2048])

    yT = dout("yT", [KC, 128, NTOK])
    nckvT = dout("nckvT", [L, 4, 128, 512])
    nkpeT = dout("nkpeT", [L, 64, 512])

    xsA = dscr("xsA", [KC, 128, NTOK], F32)
    xsB = dscr("xsB", [KC, 128, NTOK], F32)
    yaD = dscr("yaD", [8, 128, NTOK], BF16)
    ycD = dscr("ycD", [8, 128, NTOK], BF16)
    obD = dscr("obD", [16, 128, NTOK], BF16)
    vsD = dscr("vsD", [8, 128, NTOK], F32)
    mgD = dscr("mgD", [16, 128, NTOK], BF16)
    asD = dscr("asD", [64, 128, NTOK], BF16)
    PWS = (4096, 4096, 2560)
    pay_ts = [nc.dram_tensor("pay%d" % i, [128, w], F32) for i, w in enumerate(PWS)]
    payg_ts = [nc.dram_tensor("payg%d" % i, [256, w], F32) for i, w in enumerate(PWS)]

    class _Pay:
        def __init__(self, ts):
            self.aps = [t.ap() for t in ts]

        def __getitem__(self, key):
            rows, cols = key
            c0, c1 = cols.start, cols.stop
            base = 0
            for ap_, w in zip(self.aps, PWS):
                if c0 >= base and c1 <= base + w:
                    return ap_[rows, c0 - base:c1 - base]
                base += w
            raise AssertionError(("payload slice straddles pieces", c0, c1))
    pay = _Pay(pay_ts)
    payg = _Pay(payg_ts)
    dbgD = dscr("dbgD", [128, 16 * 512], F32) if debug else None

    groups = [[2 * i, 2 * i + 1] for i in range(n_cores // 2)]

    with contextlib.ExitStack() as st:
        P = Prog(nc, st)
        SBYTES = 206 * 1024
        SB = st.enter_context(nc.sbuf_tensor("SB", [128, SBYTES // 2], BF16))
        banks = [T(st.enter_context(nc.psum_tensor("ps%d" % i, [128, 512], F32))[:]) for i in range(8)]
        bank_rr = [0]

        def nbank():
            b = banks[bank_rr[0] % 8]
            bank_rr[0] += 1
            return b

        def carve(off, n, dt):
            assert off % 4 == 0
            if dt is BF16:
                assert off + 2 * n <= SBYTES, (off, n)
                return SB[:, off // 2: off // 2 + n]
            assert off + 4 * n <= SBYTES, (off, n)
            return SB[:, off // 2: off // 2 + 2 * n].bitcast(F32)

        class Alloc:
            def __init__(self, base, limit):
                self.base, self.limit, self.p = base, limit, base

            def reset(self):
                self.p = self.base

            def get(self, n, dt):
                sz = n * (2 if dt is BF16 else 4)
                sz = (sz + 31) // 32 * 32
                a = carve(self.p, n, dt)
                self.p += sz
                assert self.p <= self.limit, ("sbuf overflow", self.p, self.limit)
                return a

        pers = Alloc(0, SBYTES)
        H_ap = pers.get(KC * NTOK, BF16)
        hT = [[T(H_ap[:, kc * NTOK + tt * TT: kc * NTOK + (tt + 1) * TT]) for tt in range(NT)] for kc in range(KC)]
        small_ap = pers.get(NSM, F32)
        small_t = T(small_ap)
        modv_ap = pers.get(L * 96 * 2, F32)
        modv_t = T(modv_ap)
        gs_ap = pers.get(L * 2 * 16 * 2, F32)
        gs_t = T(gs_ap)
        onesf_ap = pers.get(128, F32)
        onesf_t = T(onesf_ap)
        onesb_ap = pers.get(128, BF16)
        onesb_t = T(onesb_ap)
        rope_ap = pers.get(NSAMP, F32)
        rope_t = T(rope_ap)
        scb_ap = pers.get(32, BF16)
        scb_t = T(scb_ap)
        eps_ap = pers.get(8, F32)[:, 0:1]
        eps_t = T(eps_ap)
        ckvP_ap = pers.get(4 * 512, BF16)
        kpeP_ap = pers.get(512, BF16)
        ckvP_t, kpeP_t = T(ckvP_ap), T(kpeP_ap)
        PH_BASE = pers.p
        ph = Alloc(PH_BASE, SBYTES)

        def scol(l, off, i=0):
            o = (l * SL if l is not None else 0) + off + i
            return small_ap[:, o:o + 1]

        def modcol(l, idx, which):
            o = (l * 96 + idx) * 2 + which
            return modv_ap[:, o:o + 1]

        def gscol(l, nrm, kc, which):
            o = ((l * 2 + nrm) * 16 + kc) * 2 + which
            return gs_ap[:, o:o + 1]

        def wtt(tt):
            return 0 if tt < 4 else 1

        def mm_group(ps_t, ps_ap, pairs, reads):
            def fn(e, pairs=pairs, ps_ap=ps_ap):
                n = len(pairs)
                ins = None
                for i, (l_, r_) in enumerate(pairs):
                    ins = e.matmul(ps_ap, l_, r_, start=(i == 0), stop=(i == n - 1))
                return ins
            P.op("pe", fn, reads=reads, writes=[ps_t])

        evac_rr = [0]

        def evac_copy(out_ap, out_ts, ps_ap, ps_t, eng=None):
            if eng is None:
                eng = ("act", "dve")[evac_rr[0] % 2]
                evac_rr[0] += 1
            if eng == "act":
                P.op("act", lambda e: e.activation(out=out_ap, in_=ps_ap, func=AF.Copy), reads=[ps_t], writes=out_ts)
            else:
                P.op("dve", lambda e: e.tensor_copy(out=out_ap, in_=ps_ap), reads=[ps_t], writes=out_ts)

        P.dma("sp", small_ap, small[:, :], writes=[small_t])
        P.dma("sp", rope_ap, ropeT[:, :], writes=[rope_t])
        P.op("dve", lambda e: e.memset(onesf_ap, 1.0), writes=[onesf_t])
        P.op("dve", lambda e: e.memset(onesb_ap, 1.0), writes=[onesb_t])
        P.op("dve", lambda e: e.memset(eps_ap, EPS), writes=[eps_t])
        P.op("dve", lambda e: e.memset(kpeP_ap, 0.0), writes=[kpeP_t])
        cv_ap = ph.get(32, F32)
        cv_t = T(cv_ap)
        P.dma("sp", cv_ap, cvec[:, :], writes=[cv_t])
        P.op("act", lambda e: e.activation(out=scb_ap, in_=cv_ap, func=AF.Silu), reads=[cv_t], writes=[scb_t])
        z_ap = ph.get(NSAMP, F32)
        z_t = T(z_ap)
        P.op("dve", lambda e: e.memset(z_ap, 0.0), writes=[z_t])
        P.dma("sp", pay[64:128, 4 * NSAMP:5 * NSAMP], z_ap[64:128, :], reads=[z_t])

        def mod_finish(l, nrms=(0, 1)):
            mv3 = modv_ap[:, l * 192:(l + 1) * 192].rearrange("p (m w) -> p m w", w=2)
            for nrm in nrms:
                sc_idx = 16 if nrm == 0 else 64
                g_off = O_N1G if nrm == 0 else O_N2G
                gv = small_ap[:, l * SL + g_off: l * SL + g_off + 16]
                gs3 = gs_ap[:, (l * 2 + nrm) * 32:(l * 2 + nrm + 1) * 32].rearrange("p (k w) -> p k w", w=2)
                for w_ in range(2):
                    P.op("dve", lambda e, w_=w_, gs3=gs3, gv=gv, sc_idx=sc_idx: e.scalar_tensor_tensor(
                        out=gs3[:, :, w_], in0=mv3[:, sc_idx:sc_idx + 16, w_], scalar=1.0, in1=gv, op0=ALU.add, op1=ALU.mult),
                        reads=[modv_t, small_t], writes=[gs_t])

        def mod_chunk(l, m, w):
            P.dma("pool", w.ap, wMod[l, m], writes=[w])
            ps = nbank()
            pairs = [(w.ap[:, kc * 128:(kc + 1) * 128], scb_ap[:, kc * 2:kc * 2 + 2]) for kc in range(KC)]
            mm_group(ps, ps.ap[:, 0:2], pairs, [w, scb_t])
            o = (l * 96 + m) * 2
            P.op("dve", lambda e, ps=ps, o=o, m=m: e.tensor_scalar(out=modv_ap[:, o:o + 2], in0=ps.ap[:, 0:2], scalar1=scol(l, O_BMOD, m), scalar2=None, op0=ALU.add),
                 reads=[ps, small_t], writes=[modv_t])

        def phase_mod(l):
            ph.reset()
            wr = Ring([ph.get(2048, BF16) for _ in range(4)])
            for m in range(32):
                mod_chunk(l, m, wr.next())
            mod_finish(l, (0,))
            P.barrier()

        def phase_norm(xsrc, l, final=False):
            ph.reset()
            xr = Ring([ph.get(KC * TT, F32) for _ in range(2)])
            sqr = Ring([ph.get(TT, F32) for _ in range(2)])
            sdr = Ring([ph.get(TT, F32) for _ in range(2)])
            tmr = Ring([ph.get(TT, F32) for _ in range(3)])
            yr = Ring([ph.get(4 * TT, F32) for _ in range(2)]) if final else None
            for tt in range(NT):
                w_ = wtt(tt)
                xt = xr.next()
                x3 = xt.ap.rearrange("p (k t) -> p k t", k=KC)
                P.dma_multi("sp", [(x3[:, q4 * 4:(q4 + 1) * 4, :],
                                    xsrc[q4 * 4:(q4 + 1) * 4, :, tt * TT:(tt + 1) * TT].rearrange("k p t -> p k t")) for q4 in range(4)], writes=[xt])
                ps = nbank()
                for kc in range(KC):
                    sq = sqr.next()
                    P.op("act", lambda e, sq=sq, kc=kc, x3=x3: e.activation(out=sq.ap, in_=x3[:, kc, :], func=AF.Square),
                         reads=[xt], writes=[sq])
                    P.op("pe", lambda e, sq=sq, kc=kc, ps=ps: e.matmul(ps.ap, onesf_ap, sq.ap, start=(kc == 0), stop=(kc == KC - 1)),
                         reads=[sq, onesf_t], writes=[ps])
                sd = sdr.next()
                P.op("act", lambda e, sd=sd, ps=ps: e.activation(out=sd.ap, in_=ps.ap, func=AF.Sqrt, scale=1.0 / D, bias=eps_ap),
                     reads=[ps], writes=[sd])
                P.op("dve", lambda e, sd=sd: e.reciprocal(out=sd.ap, in_=sd.ap), reads=[sd], writes=[sd])
                if final:
                    for q4 in range(4):
                        yt = yr.next()
                        y3 = yt.ap.rearrange("p (k t) -> p k t", k=4)
                        for k4 in range(4):
                            kc = q4 * 4 + k4
                            P.op("dve", lambda e, kc=kc, k4=k4, y3=y3, x3=x3, sd=sd: e.scalar_tensor_tensor(
                                out=y3[:, k4, :], in0=x3[:, kc, :], scalar=scol(None, O_FNG, kc), in1=sd.ap, op0=ALU.mult, op1=ALU.mult),
                                reads=[xt, sd, small_t], writes=[yt])
                        P.dma("sp", yT[q4 * 4:(q4 + 1) * 4, :, tt * TT:(tt + 1) * TT].rearrange("k p t -> p k t"), y3, reads=[yt])
                else:
                    for kc in range(KC):
                        tm = tmr.next()
                        P.op("dve", lambda e, kc=kc, tm=tm, x3=x3, sd=sd, w_=w_: e.scalar_tensor_tensor(
                            out=tm.ap, in0=x3[:, kc, :], scalar=gscol(l, 0, kc, w_), in1=sd.ap, op0=ALU.mult, op1=ALU.mult),
                            reads=[xt, sd, gs_t], writes=[tm])
                        P.op("act", lambda e, kc=kc, tm=tm, tt=tt, w_=w_: e.activation(
                            out=hT[kc][tt].ap, in_=tm.ap, func=AF.Identity, bias=modcol(l, 0 + kc, w_), scale=1.0),
                            reads=[tm, modv_t], writes=[hT[kc][tt]])
            P.barrier()

        def gemm_h(w, cb, ncols=128, col0=0):
            for tt in range(NT):
                ps = nbank()
                pairs = [(w.ap[:, kc * 128 + col0: kc * 128 + col0 + ncols], hT[kc][tt].ap) for kc in range(KC)]
                mm_group(ps, ps.ap[0:ncols, :], pairs, [w] + [hT[kc][tt] for kc in range(KC)])
                cb(tt, ps)

        att = {}

        def phase_halo(l):
            ph.reset()
            wr = Ring([ph.get(2048, BF16) for _ in range(4)])
            hcat_ap = ph.get(KC * 32, BF16)
            hcat_t = T(hcat_ap)
            for kc in range(KC):
                P.op("dve", lambda e, kc=kc: e.tensor_copy(out=hcat_ap[:, kc * 32:kc * 32 + 16], in_=hT[kc][0].ap[:, 0:16]),
                     reads=[hT[kc][0]], writes=[hcat_t])
                P.op("dve", lambda e, kc=kc: e.tensor_copy(out=hcat_ap[:, kc * 32 + 16:kc * 32 + 32], in_=hT[kc][3].ap[:, TT - 16:TT]),
                     reads=[hT[kc][3]], writes=[hcat_t])
            hA_ap = ph.get(256, F32)
            hC_ap = ph.get(256, F32)
            hA_t, hC_t = T(hA_ap), T(hC_ap)
            gtmp = Ring([ph.get(32, F32) for _ in range(2)])
            for (wsrc, first, second, func, dst_ap, dst_t) in ((wA, 0, 16, AF.Copy, hA_ap, hA_t), (wC, 0, 8, AF.Sigmoid, hC_ap, hC_t)):
                for j in range(8):
                    w1 = wr.next()
                    P.dma("pool", w1.ap, wsrc[l, first + j], writes=[w1])
                    w2 = wr.next()
                    P.dma("pool", w2.ap, wsrc[l, second + j], writes=[w2])
                    ps = nbank()
                    mm_group(ps, ps.ap[:, 0:32], [(w1.ap[:, kc * 128:(kc + 1) * 128], hcat_ap[:, kc * 32:(kc + 1) * 32]) for kc in range(KC)], [w1, hcat_t])
                    ps2 = nbank()
                    mm_group(ps2, ps2.ap[:, 0:32], [(w2.ap[:, kc * 128:(kc + 1) * 128], hcat_ap[:, kc * 32:(kc + 1) * 32]) for kc in range(KC)], [w2, hcat_t])
                    g = gtmp.next()
                    P.op("act", lambda e, g=g, ps2=ps2, func=func: e.activation(out=g.ap, in_=ps2.ap[:, 0:32], func=func), reads=[ps2], writes=[g])
                    P.op("dve", lambda e, g=g, ps=ps, j=j, dst_ap=dst_ap: e.tensor_tensor(out=dst_ap[:, j * 32:(j + 1) * 32], in0=ps.ap[:, 0:32], in1=g.ap, op=ALU.mult),
                         reads=[ps, g], writes=[dst_t])
            P.dma("sp", pay[:, 10240:10496], hA_ap, reads=[hA_t])
            P.dma("sp", pay[:, 10496:10752], hC_ap, reads=[hC_t])
            P.barrier()

        def phase_kv(l):
            ph.reset()
            wk_ = [T(ph.get(2048, BF16)) for _ in range(5)]
            for m in range(5):
                P.dma("pool", wk_[m].ap, wKV[l, m], writes=[wk_[m]])
            rawr = [Ring([ph.get(TT, F32) for _ in range(2)]) for _ in range(5)]
            sqr = Ring([ph.get(TT, F32) for _ in range(2)])
            sdr = Ring([ph.get(TT, F32) for _ in range(2)])
            cnr = Ring([ph.get(TT, F32) for _ in range(4)])
            t1r = Ring([ph.get(TT, F32) for _ in range(2)])
            t2r = Ring([ph.get(TT, F32) for _ in range(2)])
            for tt in range(NT):
                rd = [hT[kc][tt] for kc in range(KC)]
                raw = []
                for m in range(5):
                    ps = nbank()
                    mm_group(ps, ps.ap, [(wk_[m].ap[:, kc * 128:(kc + 1) * 128], hT[kc][tt].ap) for kc in range(KC)], [wk_[m]] + rd)
                    r = rawr[m].next()
                    evac_copy(r.ap, [r], ps.ap, ps)
                    raw.append(r)
                ps = nbank()
                for m in range(4):
                    sq = sqr.next()
                    P.op("act", lambda e, sq=sq, r=raw[m]: e.activation(out=sq.ap, in_=r.ap, func=AF.Square), reads=[raw[m]], writes=[sq])
                    P.op("pe", lambda e, sq=sq, m=m, ps=ps: e.matmul(ps.ap, onesf_ap, sq.ap, start=(m == 0), stop=(m == 3)), reads=[sq, onesf_t], writes=[ps])
                sd = sdr.next()
                P.op("act", lambda e, sd=sd, ps=ps: e.activation(out=sd.ap, in_=ps.ap, func=AF.Sqrt, scale=1.0 / 512, bias=eps_ap), reads=[ps, eps_t], writes=[sd])
                P.op("dve", lambda e, sd=sd: e.reciprocal(out=sd.ap, in_=sd.ap), reads=[sd], writes=[sd])
                for m in range(4):
                    cn = cnr.next()
                    P.op("dve", lambda e, cn=cn, m=m, r=raw[m], sd=sd: e.scalar_tensor_tensor(
                        out=cn.ap, in0=r.ap, scalar=scol(l, O_KVG, m), in1=sd.ap, op0=ALU.mult, op1=ALU.mult),
                        reads=[raw[m], sd, small_t], writes=[cn])
                    if tt < 4:
                        P.dma("sp", pay[:, m * NSAMP + tt * TT: m * NSAMP + (tt + 1) * TT], cn.ap, reads=[cn])
                    else:
                        P.dma("sp", nckvT[l, m], cn.ap, reads=[cn])
                        P.op("act", lambda e, cn=cn, m=m: e.activation(out=ckvP_ap[:, m * 512:(m + 1) * 512], in_=cn.ap, func=AF.Copy),
                             reads=[cn], writes=[ckvP_t])
                rk = raw[4]
                if tt == 4:
                    P.dma("sp", nkpeT[l], rk.ap[0:64, :], reads=[rk])
                    P.op("act", lambda e, rk=rk: e.activation(out=kpeP_ap[0:64, :], in_=rk.ap[0:64, :], func=AF.Copy), reads=[rk], writes=[kpeP_t])
                else:
                    t1, t2 = t1r.next(), t2r.next()
                    P.op("dve", lambda e, t1=t1, rk=rk, tt=tt: e.tensor_tensor(out=t1.ap[0:64, :], in0=rk.ap[0:64, :], in1=rope_ap[0:64, tt * TT:(tt + 1) * TT], op=ALU.mult),
                         reads=[rk, rope_t], writes=[t1])
                    P.op("dve", lambda e, t2=t2, rk=rk, tt=tt: e.tensor_tensor(out=t2.ap[0:64, :], in0=rk.ap[64:128, :], in1=rope_ap[64:128, tt * TT:(tt + 1) * TT], op=ALU.mult),
                         reads=[rk, rope_t], writes=[t2])
                    P.op("dve", lambda e, t1=t1, t2=t2: e.tensor_tensor(out=t1.ap[0:64, :], in0=t1.ap[0:64, :], in1=t2.ap[0:64, :], op=ALU.add),
                         reads=[t1, t2], writes=[t1])
                    P.dma("sp", pay[0:64, 4 * NSAMP + tt * TT: 4 * NSAMP + (tt + 1) * TT], t1.ap[0:64, :], reads=[t1])
            P.barrier()

        def phase_xchg(l):
            ph.reset()
            ckvS_ap = ph.get(4 * NKEY, BF16)
            kpeS_ap = ph.get(NKEY, BF16)
            halA_ap = ph.get(2 * 256, F32)
            halC_ap = ph.get(2 * 256, F32)
            att["ckvS"], att["kpeS"] = T(ckvS_ap), T(kpeS_ap)
            att["halA"], att["halC"] = T(halA_ap), T(halC_ap)
            att["base"] = ph.p
            for i in range(3):
                P.collective(lambda gp, i=i: gp.collective_compute("AllGather", ALU.bypass, replica_groups=groups,
                                                                   ins=[pay_ts[i].ap().opt()], outs=[payg_ts[i].ap().opt()]))
            P.barrier()
            for r in range(2):
                for m in range(4):
                    P.dma("pool", ckvS_ap[:, m * NKEY + r * NSAMP: m * NKEY + (r + 1) * NSAMP],
                          payg[r * 128:(r + 1) * 128, m * NSAMP:(m + 1) * NSAMP])
                P.dma("pool", kpeS_ap[0:64, r * NSAMP:(r + 1) * NSAMP], payg[r * 128:r * 128 + 64, 4 * NSAMP:5 * NSAMP])
                P.dma("sp", halA_ap[:, r * 256:(r + 1) * 256], payg[r * 128:(r + 1) * 128, 10240:10496])
                P.dma("sp", halC_ap[:, r * 256:(r + 1) * 256], payg[r * 128:(r + 1) * 128, 10496:10752])
            for m in range(4):
                P.dma("pool", ckvS_ap[:, m * NKEY + 4096: m * NKEY + 4608], cck[l, m])
            P.dma("pool", kpeS_ap[0:64, 4096:4608], ckp[l])
            P.op("dve", lambda e: e.memset(kpeS_ap[64:128, :], 0.0), writes=[att["kpeS"]])
            P.barrier()

        SEGS = ((0, NSAMP), (NSAMP, 256), (NSAMP + 256, 256))

        def seg_layout(padw):
            offs = []
            o = 0
            for (s0, ln) in SEGS:
                offs.append(o + padw)
                o += ln + 2 * padw
            return offs, o

        def pieces(tt, offs):
            out = []
            t0, t1 = tt * TT, (tt + 1) * TT
            for (s0, ln), bo in zip(SEGS, offs):
                a, b = max(t0, s0), min(t1, s0 + ln)
                if a < b:
                    out.append((a - t0, b - t0, bo + (a - s0)))
            return out

        def phase_a(l):
            ph.p = att["base"]
            offs, W = seg_layout(1)
            wr = Ring([ph.get(2048, BF16) for _ in range(3)])
            tp_ap = ph.get(W, F32)
            tp_t = T(tp_ap)
            y_ap = ph.get(W, F32)
            y_t = T(y_ap)
            gb_ap = ph.get(NTOK, F32)
            gbT = [T(gb_ap[:, tt * TT:(tt + 1) * TT]) for tt in range(NT)]
            gcr = Ring([ph.get(TT, F32) for _ in range(2)])
            oar = Ring([ph.get(NTOK, BF16) for _ in range(1)])
            P.op("dve", lambda e: e.memset(tp_ap, 0.0), writes=[tp_t])
            for j in range(8):
                wx, wb, wc = wr.next(), wr.next(), wr.next()
                P.dma("pool", wx.ap, wA[l, j], writes=[wx])
                P.dma("pool", wc.ap, wA[l, 16 + j], writes=[wc])
                P.dma("pool", wb.ap, wA[l, 8 + j], writes=[wb])
                for tt in range(NT):
                    rd = [hT[kc][tt] for kc in range(KC)]
                    psx, psc, psb = nbank(), nbank(), nbank()
                    mm_group(psx, psx.ap, [(wx.ap[:, kc * 128:(kc + 1) * 128], hT[kc][tt].ap) for kc in range(KC)], [wx] + rd)
                    mm_group(psc, psc.ap, [(wc.ap[:, kc * 128:(kc + 1) * 128], hT[kc][tt].ap) for kc in range(KC)], [wc] + rd)
                    mm_group(psb, psb.ap, [(wb.ap[:, kc * 128:(kc + 1) * 128], hT[kc][tt].ap) for kc in range(KC)], [wb] + rd)
                    gc = gcr.next()
                    P.op("act", lambda e, gc=gc, psc=psc: e.activation(out=gc.ap, in_=psc.ap, func=AF.Copy), reads=[psc], writes=[gc])
                    for (a, b, bo) in pieces(tt, offs):
                        P.op("dve", lambda e, a=a, b=b, bo=bo, psx=psx, gc=gc: e.tensor_tensor(
                            out=tp_ap[:, bo:bo + (b - a)], in0=psx.ap[:, a:b], in1=gc.ap[:, a:b], op=ALU.mult),
                            reads=[psx, gc], writes=[tp_t])
                    P.op("act", lambda e, tt=tt, psb=psb: e.activation(out=gbT[tt].ap, in_=psb.ap, func=AF.Copy), reads=[psb], writes=[gbT[tt]])
                hA = att["halA"].ap
                P.op("dve", lambda e, j=j: e.tensor_scalar(out=tp_ap[:, offs[0] - 1:offs[0]], in0=hA[:, 0 * 256 + j * 32 + 31: 0 * 256 + j * 32 + 32],
                                                            scalar1=scol(None, O_MASKL), scalar2=None, op0=ALU.mult), reads=[att["halA"], small_t], writes=[tp_t])
                P.op("dve", lambda e, j=j: e.tensor_scalar(out=tp_ap[:, offs[0] + NSAMP:offs[0] + NSAMP + 1], in0=hA[:, 1 * 256 + j * 32: 1 * 256 + j * 32 + 1],
                                                            scalar1=scol(None, O_MASKR), scalar2=None, op0=ALU.mult), reads=[att["halA"], small_t], writes=[tp_t])
                P.op("dve", lambda e, j=j: e.tensor_scalar(out=y_ap[:, 0:W - 2], in0=tp_ap[:, 0:W - 2], scalar1=scol(l, O_CAW, j * 3 + 0), scalar2=None, op0=ALU.mult),
                     reads=[tp_t, small_t], writes=[y_t])
                for k in (1, 2):
                    P.op("dve", lambda e, j=j, k=k: e.scalar_tensor_tensor(out=y_ap[:, 0:W - 2], in0=tp_ap[:, k:W - 2 + k], scalar=scol(l, O_CAW, j * 3 + k),
                                                                        in1=y_ap[:, 0:W - 2], op0=ALU.mult, op1=ALU.add), reads=[tp_t, y_t, small_t], writes=[y_t])
                oa = oar.next()
                for tt in range(NT):
                    for (a, b, bo) in pieces(tt, offs):
                        P.op("dve", lambda e, a=a, b=b, bo=bo, tt=tt, oa=oa: e.tensor_tensor(
                            out=oa.ap[:, tt * TT + a: tt * TT + b], in0=y_ap[:, bo - 1: bo - 1 + (b - a)], in1=gbT[tt].ap[:, a:b], op=ALU.mult),
                            reads=[y_t, gbT[tt]], writes=[oa])
                P.dma("sp", yaD[j], oa.ap, reads=[oa])
            P.barrier()

        def phase_c1(l, mod_rest=False):
            ph.p = att["base"]
            PADW = 15
            offs, W = seg_layout(PADW)
            WV = W - 2 * PADW
            wr = Ring([ph.get(2048, BF16) for _ in range(3)])
            up_ap = ph.get(W, F32)
            up_t = T(up_ap)
            vr = Ring([ph.get(WV, F32) for _ in range(2)])
            sgr = Ring([ph.get(TT, F32) for _ in range(2)])
            wmr = Ring([ph.get(2048, BF16) for _ in range(2)])
            P.op("dve", lambda e: e.memset(up_ap, 0.0), writes=[up_t])
            hC = att["halC"].ap
            for j in range(8):
                wa, wg = wr.next(), wr.next()
                P.dma("pool", wa.ap, wC[l, j], writes=[wa])
                P.dma("pool", wg.ap, wC[l, 8 + j], writes=[wg])
                for tt in range(NT):
                    rd = [hT[kc][tt] for kc in range(KC)]
                    psa, psg = nbank(), nbank()
                    mm_group(psa, psa.ap, [(wa.ap[:, kc * 128:(kc + 1) * 128], hT[kc][tt].ap) for kc in range(KC)], [wa] + rd)
                    mm_group(psg, psg.ap, [(wg.ap[:, kc * 128:(kc + 1) * 128], hT[kc][tt].ap) for kc in range(KC)], [wg] + rd)
                    sg = sgr.next()
                    P.op("act", lambda e, sg=sg, psg=psg: e.activation(out=sg.ap, in_=psg.ap, func=AF.Sigmoid), reads=[psg], writes=[sg])
                    for (a, b, bo) in pieces(tt, offs):
                        P.op("dve", lambda e, a=a, b=b, bo=bo, psa=psa, sg=sg: e.tensor_tensor(
                            out=up_ap[:, bo:bo + (b - a)], in0=psa.ap[:, a:b], in1=sg.ap[:, a:b], op=ALU.mult),
                            reads=[psa, sg], writes=[up_t])
                P.op("dve", lambda e, j=j: e.tensor_scalar(out=up_ap[:, offs[0] - 15:offs[0]], in0=hC[:, j * 32 + 17: j * 32 + 32],
                                                            scalar1=scol(None, O_MASKL), scalar2=None, op0=ALU.mult), reads=[att["halC"], small_t], writes=[up_t])
                P.op("dve", lambda e, j=j: e.tensor_scalar(out=up_ap[:, offs[0] + NSAMP:offs[0] + NSAMP + 15], in0=hC[:, 256 + j * 32: 256 + j * 32 + 15],
                                                            scalar1=scol(None, O_MASKR), scalar2=None, op0=ALU.mult), reads=[att["halC"], small_t], writes=[up_t])
                v = vr.next()
                P.op("dve", lambda e, j=j, v=v: e.tensor_scalar(out=v.ap, in0=up_ap[:, 0:WV], scalar1=scol(l, O_CDW, j * 31 + 0), scalar2=scol(l, O_CDB, j),
                                                                 op0=ALU.mult, op1=ALU.add), reads=[up_t, small_t], writes=[v])
                for k in range(1, 31):
                    P.op("dve", lambda e, j=j, k=k, v=v: e.scalar_tensor_tensor(out=v.ap, in0=up_ap[:, k:k + WV], scalar=scol(l, O_CDW, j * 31 + k),
                                                                               in1=v.ap, op0=ALU.mult, op1=ALU.add), reads=[up_t, v, small_t], writes=[v])
                if mod_rest:
                    for m in range(32 + 8 * j, 32 + 8 * j + 8):
                        mod_chunk(l, m, wmr.next())
                for (s0, ln), bo in zip(SEGS, offs):
                    P.dma("sp", vsD[j, :, s0:s0 + ln], v.ap[:, bo - PADW: bo - PADW + ln], reads=[v])
            if mod_rest:
                mod_finish(l, (1,))
            P.barrier()

        def phase_c2(l):
            ph.p = att["base"]
            vr = Ring([ph.get(8 * TT, F32) for _ in range(1)])
            sqr = Ring([ph.get(TT, F32) for _ in range(2)])
            mnr = Ring([ph.get(TT, F32) for _ in range(2)])
            rsr = Ring([ph.get(TT, F32) for _ in range(2)])
            dr = Ring([ph.get(TT, F32) for _ in range(3)])
            orr = Ring([ph.get(8 * TT, BF16) for _ in range(1)])
            for tt in range(NT):
                vt = vr.next()
                v3 = vt.ap.rearrange("p (k t) -> p k t", k=8)
                P.dma_multi("sp", [(v3[:, q * 4:(q + 1) * 4, :], vsD[q * 4:(q + 1) * 4, :, tt * TT:(tt + 1) * TT].rearrange("k p t -> p k t")) for q in range(2)], writes=[vt])
                ps1, ps2 = nbank(), nbank()
                for j in range(8):
                    P.op("pe", lambda e, j=j, ps1=ps1, v3=v3: e.matmul(ps1.ap, onesf_ap, v3[:, j, :], start=(j == 0), stop=(j == 7)), reads=[vt, onesf_t], writes=[ps1])
                    sq = sqr.next()
                    P.op("act", lambda e, j=j, sq=sq, v3=v3: e.activation(out=sq.ap, in_=v3[:, j, :], func=AF.Square), reads=[vt], writes=[sq])
                    P.op("pe", lambda e, j=j, ps2=ps2, sq=sq: e.matmul(ps2.ap, onesf_ap, sq.ap, start=(j == 0), stop=(j == 7)), reads=[sq, onesf_t], writes=[ps2])
                mn, rs = mnr.next(), rsr.next()
                P.op("act", lambda e, mn=mn, ps1=ps1: e.activation(out=mn.ap, in_=ps1.ap, func=AF.Identity, scale=1.0 / 1024), reads=[ps1], writes=[mn])
                P.op("dve", lambda e, rs=rs, mn=mn: e.tensor_tensor(out=rs.ap, in0=mn.ap, in1=mn.ap, op=ALU.mult), reads=[mn], writes=[rs])
                P.op("dve", lambda e, rs=rs, ps2=ps2: e.scalar_tensor_tensor(out=rs.ap, in0=ps2.ap, scalar=1.0 / 1024, in1=rs.ap, op0=ALU.mult, op1=ALU.subtract),
                     reads=[ps2, rs], writes=[rs])
                P.op("act", lambda e, rs=rs: e.activation(out=rs.ap, in_=rs.ap, func=AF.Sqrt, scale=1.0, bias=eps_ap), reads=[rs], writes=[rs])
                P.op("dve", lambda e, rs=rs: e.reciprocal(out=rs.ap, in_=rs.ap), reads=[rs], writes=[rs])
                ot = orr.next()
                o3 = ot.ap.rearrange("p (k t) -> p k t", k=8)
                for j in range(8):
                    d = dr.next()
                    P.op("dve", lambda e, j=j, d=d, v3=v3, mn=mn: e.tensor_tensor(out=d.ap, in0=v3[:, j, :], in1=mn.ap, op=ALU.subtract), reads=[vt, mn], writes=[d])
                    P.op("dve", lambda e, d=d, rs=rs: e.tensor_tensor(out=d.ap, in0=d.ap, in1=rs.ap, op=ALU.mult), reads=[d, rs], writes=[d])
                    P.op("act", lambda e, j=j, d=d, o3=o3: e.activation(out=o3[:, j, :], in_=d.ap, func=AF.Silu, scale=scol(l, O_CLG, j), bias=scol(l, O_CLB, j)),
                         reads=[d, small_t], writes=[ot])
                P.dma_multi("sp", [(ycD[q * 4:(q + 1) * 4, :, tt * TT:(tt + 1) * TT].rearrange("k p t -> p k t"), o3[:, q * 4:(q + 1) * 4, :]) for q in range(2)], reads=[ot])
            P.barrier()

        def phase_b(l):
            ph.p = att["base"]
            ckvS, kpeS, ckvP, kpeP = att["ckvS"], att["kpeS"], ckvP_t, kpeP_t
            wr = Ring([ph.get(2048, BF16) for _ in range(2)])
            wkr = Ring([ph.get(512, BF16) for _ in range(2)])
            wvr = Ring([ph.get(512, BF16) for _ in range(2)])
            qn_ap = ph.get(NTOK, BF16)
            qnT = [T(qn_ap[:, tt * TT:(tt + 1) * TT]) for tt in range(NT)]
            qp_ap = ph.get(NTOK, BF16)
            qpT = [T(qp_ap[:, tt * TT:(tt + 1) * TT]) for tt in range(NT)]
            P.op("dve", lambda e: e.memset(qp_ap[64:128, :], 0.0), writes=qpT)
            kh_ap = ph.get(NKEY + 512, BF16)
            khT = [T(kh_ap[:, i * TT:(i + 1) * TT]) for i in range(10)]
            vh_ap = ph.get(NKEY + 512, BF16)
            vhT = [T(vh_ap[:, i * TT:(i + 1) * TT]) for i in range(10)]
            pr = Ring([ph.get(TT, BF16) for _ in range(5)])
            rcr = Ring([ph.get(TT, F32) for _ in range(2)])
            onr = Ring([ph.get(TT, BF16) for _ in range(2)])
            t1r = Ring([ph.get(TT, F32) for _ in range(1)])
            t2r = Ring([ph.get(TT, F32) for _ in range(1)])
            accd, accp = t1r.t[0], t2r.t[0]
            sb_, ob_, smb_ = [banks[0], banks[1], banks[2]], [banks[3], banks[4]], [banks[5]]
            gb_ = [banks[6], banks[7]]
            gi = [0]

            def gbank():
                b = gb_[gi[0] % 2]
                gi[0] += 1
                return b
            si = [0]
            oi = [0]
            for h in range(16):
                wq = wr.next()
                P.dma("pool", wq.ap, wQn[l, h], writes=[wq])
                wqr = wr.next()
                P.dma("pool", wqr.ap, wQr[l, h], writes=[wqr])
                wk = wkr.next()
                P.dma("pool", wk.ap, wKb[l, h], writes=[wk])
                wv = wvr.next()
                P.dma("pool", wv.ap, wVb[l, h], writes=[wv])
                for tt in range(NT):
                    rd = [hT[kc][tt] for kc in range(KC)]
                    ps = gbank()
                    mm_group(ps, ps.ap, [(wq.ap[:, kc * 128:(kc + 1) * 128], hT[kc][tt].ap) for kc in range(KC)], [wq] + rd)
                    evac_copy(qnT[tt].ap, [qnT[tt]], ps.ap, ps)
                    ps = gbank()
                    mm_group(ps, ps.ap, [(wqr.ap[:, kc * 128:(kc + 1) * 128], hT[kc][tt].ap) for kc in range(KC)], [wqr] + rd)
                    if tt == 4:
                        evac_copy(qpT[tt].ap[0:64, :], [qpT[tt]], ps.ap[0:64, :], ps)
                    else:
                        t1, t2 = t1r.next(), t2r.next()
                        P.op("dve", lambda e, t1=t1, ps=ps, tt=tt: e.tensor_tensor(out=t1.ap[0:64, :], in0=ps.ap[0:64, :], in1=rope_ap[0:64, tt * TT:(tt + 1) * TT], op=ALU.mult),
                             reads=[ps, rope_t], writes=[t1])
                        P.op("dve", lambda e, t2=t2, ps=ps, tt=tt: e.tensor_tensor(out=t2.ap[0:64, :], in0=ps.ap[64:128, :], in1=rope_ap[64:128, tt * TT:(tt + 1) * TT], op=ALU.mult),
                             reads=[ps, rope_t], writes=[t2])
                        P.op("dve", lambda e, t1=t1, t2=t2, tt=tt: e.tensor_tensor(out=qpT[tt].ap[0:64, :], in0=t1.ap[0:64, :], in1=t2.ap[0:64, :], op=ALU.add),
                             reads=[t1, t2], writes=[qpT[tt]])
                for kt in range(10):
                    ps = gbank()
                    if kt < 9:
                        pairs = [(wk.ap[:, kc * 128:(kc + 1) * 128], ckvS.ap[:, kc * NKEY + kt * TT: kc * NKEY + (kt + 1) * TT]) for kc in range(4)]
                        rd = [wk, ckvS]
                    else:
                        pairs = [(wk.ap[:, kc * 128:(kc + 1) * 128], ckvP.ap[:, kc * 512:(kc + 1) * 512]) for kc in range(4)]
                        rd = [wk, ckvP]
                    mm_group(ps, ps.ap, pairs, rd)
                    evac_copy(khT[kt].ap, [khT[kt]], ps.ap, ps)
                for kt in range(10):
                    ps = gbank()

                    def fn(e, kt=kt, ps=ps, wv=wv):
                        ins = None
                        for q in range(4):
                            for kc in range(4):
                                if kt < 9:
                                    lhsT = ckvS.ap[:, kc * NKEY + kt * TT + q * 128: kc * NKEY + kt * TT + (q + 1) * 128]
                                else:
                                    lhsT = ckvP.ap[:, kc * 512 + q * 128: kc * 512 + (q + 1) * 128]
                                ins = e.matmul(ps.ap[:, q * 128:(q + 1) * 128], lhsT, wv.ap[:, kc * 128:(kc + 1) * 128], start=(kc == 0), stop=(kc == 3))
                        return ins
                    P.op("pe", fn, reads=[wv, ckvS if kt < 9 else ckvP], writes=[ps])
                    evac_copy(vhT[kt].ap, [vhT[kt]], ps.ap, ps)
                jobs = []
                for qt in range(4):
                    jobs.append((qt, 0, TT, [(kc, False) for kc in range(36)]))
                for pb in range(2):
                    jobs.append((4, pb * 256, 256, [(pb * 2 + q, True) for q in range(2)]))
                for (qt, q0, qw, klist) in jobs:
                    O = ob_[oi[0] % 2]
                    SM = smb_[0]
                    oi[0] += 1
                    nk = len(klist)

                    def emit_s(i, klist=klist, qt=qt, q0=q0, qw=qw):
                        kc, is_p = klist[i]
                        S = sb_[si[0] % 3]
                        si[0] += 1
                        if is_p:
                            lk = khT[9].ap[:, kc * 128:(kc + 1) * 128]
                            lp = kpeP.ap[:, kc * 128:(kc + 1) * 128]
                            rd = [khT[9], kpeP]
                        else:
                            lk = khT[kc // 4].ap[:, (kc % 4) * 128:(kc % 4 + 1) * 128]
                            lp = kpeS.ap[:, kc * 128:(kc + 1) * 128]
                            rd = [khT[kc // 4], kpeS]
                        pairs = [(lk, qnT[qt].ap[:, q0:q0 + qw]), (lp, qpT[qt].ap[:, q0:q0 + qw])]
                        mm_group(S, S.ap[:, 0:qw], pairs, rd + [qnT[qt], qpT[qt]])
                        p = pr.next()
                        P.op("act", lambda e, p=p, S=S, qw=qw: e.activation(out=p.ap[:, 0:qw], in_=S.ap[:, 0:qw], func=AF.Exp, scale=ATTN_SCALE), reads=[S], writes=[p])
                        return p

                    def emit_o(i, p, klist=klist, O=O, SM=SM, qw=qw, nk=nk):
                        kc, is_p = klist[i]
                        if is_p:
                            lv = vhT[9].ap[:, kc * 128:(kc + 1) * 128]
                            rd = [vhT[9]]
                        else:
                            lv = vhT[kc // 4].ap[:, (kc % 4) * 128:(kc % 4 + 1) * 128]
                            rd = [vhT[kc // 4]]

                        def fn(e, lv=lv, p=p, i=i, O=O, qw=qw, nk=nk):
                            return e.matmul(O.ap[:, 0:qw], lv, p.ap[:, 0:qw], start=(i == 0), stop=(i == nk - 1))
                        P.op("pe", fn, reads=rd + [p], writes=[O])
                        eng_, acc = ("dve", accd) if i % 2 == 0 else ("pool", accp)
                        if i < 2:
                            P.op(eng_, lambda e, p=p, acc=acc, qw=qw: e.tensor_copy(out=acc.ap[:, 0:qw], in_=p.ap[:, 0:qw]), reads=[p], writes=[acc])
                        else:
                            P.op(eng_, lambda e, p=p, acc=acc, qw=qw: e.tensor_tensor(out=acc.ap[:, 0:qw], in0=acc.ap[:, 0:qw], in1=p.ap[:, 0:qw], op=ALU.add),
                                 reads=[p, acc], writes=[acc])
                    pend = []
                    for i in range(nk):
                        pend.append((i, emit_s(i)))
                        if len(pend) > 2:
                            i0, p0 = pend.pop(0)
                            emit_o(i0, p0)
                    for (i0, p0) in pend:
                        emit_o(i0, p0)
                    P.op("pe", lambda e, SM=SM, qw=qw: e.matmul(SM.ap[:, 0:qw], onesf_ap, accd.ap[:, 0:qw], start=True, stop=False), reads=[accd, onesf_t], writes=[SM])
                    P.op("pe", lambda e, SM=SM, qw=qw: e.matmul(SM.ap[:, 0:qw], onesf_ap, accp.ap[:, 0:qw], start=False, stop=True), reads=[accp, onesf_t], writes=[SM])
                    rc = rcr.next()
                    P.op("dve", lambda e, rc=rc, SM=SM, qw=qw: e.reciprocal(out=rc.ap[:, 0:qw], in_=SM.ap[:, 0:qw]), reads=[SM], writes=[rc])
                    on = onr.next()
                    P.op("dve", lambda e, rc=rc, on=on, O=O, qw=qw: e.tensor_tensor(out=on.ap[:, 0:qw], in0=O.ap[:, 0:qw], in1=rc.ap[:, 0:qw], op=ALU.mult), reads=[O, rc], writes=[on])
                    P.dma("sp", obD[h, :, qt * TT + q0: qt * TT + q0 + qw], on.ap[:, 0:qw], reads=[on])
            P.barrier()

        def phase_m1(l):
            ph.reset()
            wgr = Ring([ph.get(2048, BF16) for _ in range(6)])
            war = Ring([ph.get(1024, BF16) for _ in range(2)])
            wor = Ring([ph.get(2048, BF16) for _ in range(2)])
            wcr = Ring([ph.get(1024, BF16) for _ in range(2)])
            ya_ap = ph.get(8 * TT, BF16)
            ob_ap = ph.get(16 * TT, BF16)
            yc_ap = ph.get(8 * TT, BF16)
            ya_t, ob_t, yc_t = T(ya_ap), T(ob_ap), T(yc_ap)
            sgr = Ring([ph.get(TT, F32) for _ in range(4)])
            acr = Ring([ph.get(TT, F32) for _ in range(3)])
            mgr = Ring([ph.get(TT, BF16) for _ in range(3)])
            for tt in range(NT):
                tsl = slice(tt * TT, (tt + 1) * TT)
                for (dst, t_, src, n) in ((ya_ap, ya_t, yaD, 8), (ob_ap, ob_t, obD, 16), (yc_ap, yc_t, ycD, 8)):
                    d3 = dst.rearrange("p (k t) -> p k t", k=n)
                    P.dma_multi("sp", [(d3[:, q * 4:(q + 1) * 4, :], src[q * 4:(q + 1) * 4, :, tsl].rearrange("k p t -> p k t")) for q in range(n // 4)], writes=[t_])
                rdh = [hT[kc][tt] for kc in range(KC)]
                for m in range(16):
                    wg = [wgr.next() for _ in range(3)]
                    for b_ in range(3):
                        P.dma("pool", wg[b_].ap, wG[l, b_ * 16 + m], writes=[wg[b_]])
                    wa, wo, wc = war.next(), wor.next(), wcr.next()
                    P.dma("pool", wa.ap, wOA[l, m], writes=[wa])
                    P.dma("pool", wo.ap, wO[l, m], writes=[wo])
                    P.dma("pool", wc.ap, wOC[l, m], writes=[wc])
                    psg = [nbank() for _ in range(3)]
                    for b_ in range(3):
                        mm_group(psg[b_], psg[b_].ap, [(wg[b_].ap[:, kc * 128:(kc + 1) * 128], hT[kc][tt].ap) for kc in range(KC)], [wg[b_]] + rdh)
                    psy = [nbank() for _ in range(3)]
                    mm_group(psy[0], psy[0].ap, [(wa.ap[:, kc * 128:(kc + 1) * 128], ya_ap[:, kc * TT:(kc + 1) * TT]) for kc in range(8)], [wa, ya_t])
                    mm_group(psy[1], psy[1].ap, [(wo.ap[:, kc * 128:(kc + 1) * 128], ob_ap[:, kc * TT:(kc + 1) * TT]) for kc in range(16)], [wo, ob_t])
                    mm_group(psy[2], psy[2].ap, [(wc.ap[:, kc * 128:(kc + 1) * 128], yc_ap[:, kc * TT:(kc + 1) * TT]) for kc in range(8)], [wc, yc_t])
                    sg = [sgr.next() for _ in range(3)]
                    for b_ in range(3):
                        P.op("act", lambda e, b_=b_, sg=sg, psg=psg: e.activation(out=sg[b_].ap, in_=psg[b_].ap, func=AF.Sigmoid), reads=[psg[b_]], writes=[sg[b_]])
                    ac = acr.next()
                    P.op("dve", lambda e, ac=ac, psy=psy, sg=sg: e.tensor_tensor(out=ac.ap, in0=psy[0].ap, in1=sg[0].ap, op=ALU.mult), reads=[psy[0], sg[0]], writes=[ac])
                    P.op("dve", lambda e, psy=psy, sg=sg: e.tensor_tensor(out=sg[1].ap, in0=psy[1].ap, in1=sg[1].ap, op=ALU.mult), reads=[psy[1], sg[1]], writes=[sg[1]])
                    P.op("dve", lambda e, psy=psy, sg=sg: e.tensor_tensor(out=sg[2].ap, in0=psy[2].ap, in1=sg[2].ap, op=ALU.mult), reads=[psy[2], sg[2]], writes=[sg[2]])
                    P.op("dve", lambda e, ac=ac, sg=sg: e.tensor_tensor(out=ac.ap, in0=ac.ap, in1=sg[1].ap, op=ALU.add), reads=[ac, sg[1]], writes=[ac])
                    mg = mgr.next()
                    P.op("dve", lambda e, ac=ac, sg=sg, mg=mg: e.tensor_tensor(out=mg.ap, in0=ac.ap, in1=sg[2].ap, op=ALU.add), reads=[ac, sg[2]], writes=[mg])
                    P.dma("sp", mgD[m, :, tsl], mg.ap, reads=[mg])
            P.barrier()

        def phase_m2(l, xsrc, xdst):
            ph.reset()
            wr = Ring([ph.get(2048, BF16) for _ in range(3)])
            mr = Ring([ph.get(KC * TT, BF16) for _ in range(2)])
            xr = Ring([ph.get(KC * TT, F32) for _ in range(1)])
            sqr = Ring([ph.get(TT, F32) for _ in range(3)])
            sdr = Ring([ph.get(TT, F32) for _ in range(2)])
            tmr = Ring([ph.get(TT, F32) for _ in range(3)])
            for tt in range(NT):
                w_ = wtt(tt)
                tsl = slice(tt * TT, (tt + 1) * TT)
                mt, xt = mr.next(), xr.next()
                m3 = mt.ap.rearrange("p (k t) -> p k t", k=KC)
                x3 = xt.ap.rearrange("p (k t) -> p k t", k=KC)
                P.dma_multi("sp", [(m3[:, q * 4:(q + 1) * 4, :], mgD[q * 4:(q + 1) * 4, :, tsl].rearrange("k p t -> p k t")) for q in range(4)], writes=[mt])
                P.dma_multi("sp", [(x3[:, q * 4:(q + 1) * 4, :], xsrc[q * 4:(q + 1) * 4, :, tsl].rearrange("k p t -> p k t")) for q in range(4)], writes=[xt])
                pss = banks[7]
                pend_st = []
                for m in range(16):
                    w = wr.next()
                    P.dma("pool", w.ap, wM[l, m], writes=[w])
                    ps = banks[m % 7]
                    mm_group(ps, ps.ap, [(w.ap[:, kc * 128:(kc + 1) * 128], m3[:, kc, :]) for kc in range(KC)], [w, mt])
                    P.op("dve", lambda e, m=m, ps=ps, x3=x3, w_=w_: e.scalar_tensor_tensor(out=x3[:, m, :], in0=ps.ap, scalar=modcol(l, 32 + m, w_), in1=x3[:, m, :],
                                                                                      op0=ALU.mult, op1=ALU.add), reads=[ps, xt, modv_t], writes=[xt])
                    sq = sqr.next()
                    P.op("act", lambda e, m=m, sq=sq, x3=x3: e.activation(out=sq.ap, in_=x3[:, m, :], func=AF.Square), reads=[xt], writes=[sq])
                    pend_st.append((m, sq))
                    if len(pend_st) > 1:
                        m0, sq0 = pend_st.pop(0)
                        P.op("pe", lambda e, m0=m0, sq0=sq0, pss=pss: e.matmul(pss.ap, onesf_ap, sq0.ap, start=(m0 == 0), stop=(m0 == 15)), reads=[sq0, onesf_t], writes=[pss])
                for (m0, sq0) in pend_st:
                    P.op("pe", lambda e, m0=m0, sq0=sq0, pss=pss: e.matmul(pss.ap, onesf_ap, sq0.ap, start=(m0 == 0), stop=(m0 == 15)), reads=[sq0, onesf_t], writes=[pss])
                P.dma_multi("sp", [(xdst[q * 4:(q + 1) * 4, :, tsl].rearrange("k p t -> p k t"), x3[:, q * 4:(q + 1) * 4, :]) for q in range(4)], reads=[xt])
                sd = sdr.next()
                P.op("act", lambda e, sd=sd, pss=pss: e.activation(out=sd.ap, in_=pss.ap, func=AF.Sqrt, scale=1.0 / D, bias=eps_ap), reads=[pss], writes=[sd])
                P.op("dve", lambda e, sd=sd: e.reciprocal(out=sd.ap, in_=sd.ap), reads=[sd], writes=[sd])
                for kc in range(KC):
                    tm = tmr.next()
                    P.op("dve", lambda e, kc=kc, tm=tm, x3=x3, sd=sd, w_=w_: e.scalar_tensor_tensor(
                        out=tm.ap, in0=x3[:, kc, :], scalar=gscol(l, 1, kc, w_), in1=sd.ap, op0=ALU.mult, op1=ALU.mult), reads=[xt, sd, gs_t], writes=[tm])
                    P.op("act", lambda e, kc=kc, tm=tm, tt=tt, w_=w_: e.activation(out=hT[kc][tt].ap, in_=tm.ap, func=AF.Identity, bias=modcol(l, 48 + kc, w_), scale=1.0),
                         reads=[tm, modv_t], writes=[hT[kc][tt]])
            P.barrier()

        def phase_f1(l, mod_next=None):
            ph.reset()
            wr = Ring([ph.get(2048, BF16) for _ in range(4)])
            wmr = Ring([ph.get(2048, BF16) for _ in range(3)])
            mod_i = [0]
            rr_ = Ring([ph.get(TT, F32) for _ in range(3)])
            ar = Ring([ph.get(NTOK, BF16) for _ in range(3)])
            for f in range(64):
                w = wr.next()
                P.dma("pool", w.ap, wF1[l, f], writes=[w])
                at = ar.next()
                for tt in range(NT):
                    ps = nbank()
                    mm_group(ps, ps.ap, [(w.ap[:, kc * 128:(kc + 1) * 128], hT[kc][tt].ap) for kc in range(KC)], [w] + [hT[kc][tt] for kc in range(KC)])
                    r = rr_.next()
                    P.op("act", lambda e, r=r, ps=ps: e.activation(out=r.ap, in_=ps.ap, func=AF.Relu), reads=[ps], writes=[r])
                    P.op("dve", lambda e, r=r, at=at, tt=tt: e.tensor_tensor(out=at.ap[:, tt * TT:(tt + 1) * TT], in0=r.ap, in1=r.ap, op=ALU.mult), reads=[r], writes=[at])
                P.dma("sp", asD[f], at.ap, reads=[at])
                if mod_next is not None:
                    while mod_i[0] < 96 and mod_i[0] < (f + 1) * 1.5:
                        mod_chunk(mod_next, mod_i[0], wmr.next())
                        mod_i[0] += 1
            if mod_next is not None:
                mod_finish(mod_next)
            P.barrier()

        def phase_f2(l, xsrc, xdst):
            ph.reset()
            hreg = Alloc(0, KC * NTOK * 2)
            wr = [[T(hreg.get(16 * 512, BF16)) for _ in range(4)], [T(ph.get(16 * 512, BF16)) for _ in range(4)]]
            ar = Ring([ph.get(8 * TT, BF16) for _ in range(3)])
            xr = Ring([ph.get(TT, F32) for _ in range(8)])

            def load_w(g):
                for q in range(4):
                    P.dma("pool", wr[g % 2][q].ap, wF2[l, g, :, q * 16 * 512:(q + 1) * 16 * 512], writes=[wr[g % 2][q]])
            load_w(0)
            for g in range(4):
                wq = wr[g % 2]
                if g + 1 < 4:
                    load_w(g + 1)
                for tt in range(NT):
                    tsl = slice(tt * TT, (tt + 1) * TT)
                    xts = []
                    for mi in range(4):
                        xt = xr.next()
                        P.dma("sp", xt.ap, xsrc[g * 4 + mi, :, tsl], writes=[xt])
                        xts.append(xt)
                    pb = [nbank() for _ in range(4)]
                    for blk in range(8):
                        at = ar.next()
                        a3 = at.ap.rearrange("p (k t) -> p k t", k=8)
                        P.dma_multi("sp", [(a3[:, q * 4:(q + 1) * 4, :], asD[blk * 8 + q * 4: blk * 8 + (q + 1) * 4, :, tsl].rearrange("k p t -> p k t")) for q in range(2)], writes=[at])
                        wt = wq[blk // 2]

                        def fn(e, blk=blk, a3=a3, wt=wt, pb=pb):
                            ins = None
                            for k8 in range(8):
                                kc = blk * 8 + k8
                                kl = kc % 16
                                for mi in range(4):
                                    ins = e.matmul(pb[mi].ap, wt.ap[:, kl * 512 + mi * 128: kl * 512 + (mi + 1) * 128], a3[:, k8, :], start=(kc == 0), stop=(kc == 63))
                            return ins
                        P.op("pe", fn, reads=[at, wt], writes=pb)
                    for mi in range(4):
                        m = g * 4 + mi
                        xt = xts[mi]
                        P.op("dve", lambda e, m=m, mi=mi, xt=xt, pb=pb, tt=tt: e.scalar_tensor_tensor(out=xt.ap, in0=pb[mi].ap, scalar=modcol(l, 80 + m, wtt(tt)), in1=xt.ap,
                                                                                           op0=ALU.mult, op1=ALU.add), reads=[pb[mi], xt, modv_t], writes=[xt])
                        P.dma("act", xdst[m, :, tsl], xt.ap, reads=[xt])
            P.barrier()

        P.barrier()
        xcur = xT
        for l in range(L):
            plist = [(lambda: phase_mod(l)) if l == 0 else (lambda: None), lambda: phase_norm(xcur, l), lambda: phase_halo(l), lambda: phase_kv(l), lambda: phase_xchg(l),
                     lambda: phase_a(l), lambda: phase_c1(l, mod_rest=(l == 0)), lambda: phase_c2(l), lambda: phase_b(l), lambda: phase_m1(l),
                     lambda: phase_m2(l, xcur, xsA), lambda: phase_f1(l, l + 1 if l + 1 < L else None), lambda: phase_f2(l, xsA, xsB)]
            for i, f in enumerate(plist):
                if i < stop:
                    f()
            xcur = xsB
        if stop >= 99:
            phase_norm(xcur, None, final=True)
        P.barrier()
        with nc.Block() as block:
            P.emit(block)
    return nc


def _lay(W):
    K, N = W.shape
    return np.ascontiguousarray(W.reshape(K // 128, 128, N // 128, 128).transpose(2, 1, 0, 3)).reshape(N // 128, 128, K)


_X1 = [a * 32 + f for a in range(2) for f in range(16)]
_X2 = [a * 32 + 16 + f for a in range(2) for f in range(16)]
_ROPE_COLS = _X1 + _X2 + _X2 + _X1
_PERM = _X1 + _X2


def prep_shared(inp, L=DEPTH):
    w_in = np.asarray(inp["w_in"], np.float32)
    sh = {}
    sh["wA"] = np.stack([_lay(w_in[l][:, 0:3072]) for l in range(L)])
    qn_cols = np.concatenate([OFF_Q + h * 192 + np.arange(128) for h in range(16)])
    qr_cols = np.concatenate([OFF_Q + h * 192 + 128 + np.array(_ROPE_COLS) for h in range(16)])
    sh["wQn"] = np.stack([_lay(w_in[l][:, qn_cols]) for l in range(L)])
    sh["wQr"] = np.stack([_lay(w_in[l][:, qr_cols]) for l in range(L)])
    kv_cols = np.concatenate([OFF_KV + np.arange(512), OFF_KV + 512 + np.array(_ROPE_COLS)])
    sh["wKV"] = np.stack([_lay(w_in[l][:, kv_cols]) for l in range(L)])
    sh["wC"] = np.stack([_lay(w_in[l][:, OFF_CONF:OFF_CONF + 2048]) for l in range(L)])
    sh["wG"] = np.stack([_lay(w_in[l][:, OFF_GATE:OFF_GATE + 6144]) for l in range(L)])
    sh["wOA"] = np.stack([_lay(np.asarray(inp["w_out_a"][l], np.float32)) for l in range(L)])
    sh["wO"] = np.stack([_lay(np.asarray(inp["w_o_attn"][l], np.float32).reshape(2048, 2048)) for l in range(L)])
    sh["wOC"] = np.stack([_lay(np.asarray(inp["w_out_c"][l], np.float32)) for l in range(L)])
    sh["wM"] = np.stack([_lay(np.asarray(inp["w_merge"][l], np.float32)) for l in range(L)])
    sh["wF1"] = np.stack([_lay(np.asarray(inp["w_ff1"][l], np.float32)) for l in range(L)])
    sh["wF2"] = np.stack([np.ascontiguousarray(np.asarray(inp["w_ff2"][l], np.float32).reshape(64, 128, 4, 512).transpose(2, 1, 0, 3)).reshape(4, 128, 64 * 512)
                          for l in range(L)])
    wkvb = np.asarray(inp["w_kv_b"], np.float32)
    sh["wKb"] = np.stack([np.concatenate([_lay(wkvb[l][:, h, 0:128]) for h in range(16)]) for l in range(L)])
    sh["wVb"] = np.stack([np.concatenate([_lay(wkvb[l][:, h, 128:256]) for h in range(16)]) for l in range(L)])
    sh["wMod"] = np.stack([_lay(np.asarray(inp["w_mod"][l], np.float32)) for l in range(L)])
    return sh


def _fm(v):
    v = np.asarray(v, np.float32)
    return np.ascontiguousarray(v.reshape(-1, 128).T)


def prep_core(inp, c, L=DEPTH):
    b, half = c // 2, c % 2
    xs = np.asarray(inp["x_sample"][b, half * NSAMP:(half + 1) * NSAMP], np.float32)
    xp = np.asarray(inp["x_prompt"][2 * c:2 * c + 2], np.float32).reshape(512, D)
    xtok = np.concatenate([xs, xp], 0)
    d = {}
    d["xT"] = np.ascontiguousarray(xtok.T).reshape(KC, 128, NTOK)
    cc = np.asarray(inp["cache_ckv"][b, :L], np.float32)
    d["cck"] = np.ascontiguousarray(cc.transpose(0, 2, 1)).reshape(L, 4, 128, 512)
    ck = np.asarray(inp["cache_kpe"][b, :L], np.float32)
    d["ckp"] = np.ascontiguousarray(ck[:, :, _PERM].transpose(0, 2, 1))
    cv = np.stack([np.asarray(inp["c"][b], np.float32), np.asarray(inp["c_ctx"], np.float32)], -1)
    d["cvec"] = np.ascontiguousarray(cv.reshape(KC, 128, 2).transpose(1, 0, 2)).reshape(128, 32)
    t = np.arange(NSAMP, dtype=np.int64) + half * NSAMP
    pos = np.stack([(t // 64).astype(np.float32), (t % 64).astype(np.float32)], 0)
    inv = (np.float32(10000.0) ** (-np.arange(16, dtype=np.float32) / np.float32(16))).astype(np.float32)
    ang = (pos[:, None, :] * inv[None, :, None]).astype(np.float32)
    cs, sn = np.cos(ang).reshape(32, NSAMP), np.sin(ang).reshape(32, NSAMP)
    d["ropeT"] = np.ascontiguousarray(np.concatenate([cs, cs, -sn, sn], 0).astype(np.float32))
    sm = np.zeros((128, NSM), np.float32)
    for l in range(L):
        o = l * SL
        sm[:, o + O_N1G:o + O_N1G + 16] = _fm(inp["norm1_g"][l])
        sm[:, o + O_N2G:o + O_N2G + 16] = _fm(inp["norm2_g"][l])
        sm[:, o + O_BMOD:o + O_BMOD + 96] = _fm(inp["b_mod"][l])
        sm[:, o + O_KVG:o + O_KVG + 4] = _fm(inp["kv_norm_g"][l])
        caw = np.asarray(inp["conv_a_w"][l], np.float32)
        sm[:, o + O_CAW:o + O_CAW + 24] = caw.reshape(3, 8, 128).transpose(2, 1, 0).reshape(128, 24)
        cdw = np.asarray(inp["conf_dw_w"][l], np.float32)
        sm[:, o + O_CDW:o + O_CDW + 248] = cdw.reshape(31, 8, 128).transpose(2, 1, 0).reshape(128, 248)
        sm[:, o + O_CDB:o + O_CDB + 8] = _fm(inp["conf_dw_b"][l])
        sm[:, o + O_CLG:o + O_CLG + 8] = _fm(inp["conf_ln_g"][l])
        sm[:, o + O_CLB:o + O_CLB + 8] = _fm(inp["conf_ln_b"][l])
    sm[:, O_FNG:O_FNG + 16] = _fm(inp["final_norm_g"])
    sm[:, O_MASKL] = 1.0 if half == 1 else 0.0
    sm[:, O_MASKR] = 1.0 if half == 0 else 0.0
    d["small"] = sm
    return d


def assemble(results, n_cores=8, L=DEPTH):
    nb_p, nb_s = 2 * n_cores, n_cores // 2
    y_prompt = np.zeros((nb_p, 256, D), np.float32)
    y_sample = np.zeros((nb_s, 4096, D), np.float32)
    new_ckv = np.zeros((nb_p, L, 256, 512), np.float32)
    new_kpe = np.zeros((nb_p, L, 256, 64), np.float32)
    inv_perm = np.argsort(np.array(_PERM))
    for c in range(n_cores):
        r = results[c]
        b, half = c // 2, c % 2
        ytok = np.asarray(r["yT"]).reshape(D, NTOK).T
        y_sample[b, half * NSAMP:(half + 1) * NSAMP] = ytok[0:NSAMP]
        y_prompt[2 * c] = ytok[NSAMP:NSAMP + 256]
        y_prompt[2 * c + 1] = ytok[NSAMP + 256:]
        ck = np.asarray(r["nckvT"]).reshape(L, 512, 512)
        kp = np.asarray(r["nkpeT"])
        for i in range(2):
            new_ckv[2 * c + i] = ck[:, :, i * 256:(i + 1) * 256].transpose(0, 2, 1)
            new_kpe[2 * c + i] = kp[:, inv_perm, i * 256:(i + 1) * 256].transpose(0, 2, 1)
    return y_prompt, y_sample, new_ckv, new_kpe


_NC_CACHE = {}


def kernel(**inputs):
    L, n_cores = DEPTH, 8
    key = (L, n_cores)
    if key not in _NC_CACHE:
        _NC_CACHE[key] = build_program(L, n_cores)
    nc = _NC_CACHE[key]
    sh = prep_shared(inputs, L)
    in_maps = []
    for c in range(n_cores):
        d = prep_core(inputs, c, L)
        d.update(sh)
        in_maps.append(d)
    res = run_bass_kernel_spmd(nc, in_maps, core_ids=list(range(n_cores)))
    return assemble(res.results, n_cores, L)
```
